# Optimizing a Trainium2 kernel written in Bass

```python
import math
import jax, jax.numpy as jnp
from jax import lax
import numpy as np

D_MODEL = 1024
BATCH = 4
SEQ = 4096
DEPTH = 4

N_A_LAYERS = DEPTH // 2
N_B_LAYERS = DEPTH - N_A_LAYERS
CONV_WIDTH = 31
DIFF_HEADS = 8
DIFF_HEAD_DIM = 64
DIFF_V_DIM = 2 * DIFF_HEAD_DIM
ROT_DIM = DIFF_HEAD_DIM // 4
ROPE_THETA = 500000.0
Q_BLOCK = 128
MEM_TOKENS = 256
MEM_HEADS = 4
MEM_HEAD_DIM = D_MODEL // MEM_HEADS
FFN_HIDDEN = -(-(8 * D_MODEL) // (3 * 256)) * 256
RMS_EPS = 1e-6
LN_EPS = 1e-5
SUBLN_EPS = 1e-5

kernel_name = "yoco_conformer_conv_diff_attn_memory_trunk"


def rms_norm(x, g, eps=RMS_EPS):
    xf = x.astype(jnp.float32)
    y = xf * lax.rsqrt(jnp.mean(xf * xf, axis=-1, keepdims=True) + eps)
    return (y * g.astype(jnp.float32)).astype(x.dtype)


def layer_norm(x, g, b, eps=LN_EPS):
    xf = x.astype(jnp.float32)
    mu = jnp.mean(xf, axis=-1, keepdims=True)
    xc = xf - mu
    y = xc * lax.rsqrt(jnp.mean(xc * xc, axis=-1, keepdims=True) + eps)
    return (y * g.astype(jnp.float32) + b.astype(jnp.float32)).astype(x.dtype)


def rope_tables(positions):
    inv_freq = ROPE_THETA ** (-jnp.arange(0, ROT_DIM, 2, dtype=jnp.float32) / ROT_DIM)
    ang = positions.astype(jnp.float32)[..., None] * inv_freq
    return jnp.cos(ang), jnp.sin(ang)


def apply_partial_rope(t, cos, sin):
    half = ROT_DIM // 2
    tf = t.astype(jnp.float32)
    r1 = tf[..., :half]
    r2 = tf[..., half:ROT_DIM]
    c = cos[:, :, None, None, :]
    s = sin[:, :, None, None, :]
    out = jnp.concatenate([r1 * c - r2 * s, r1 * s + r2 * c, tf[..., ROT_DIM:]], axis=-1)
    return out.astype(t.dtype)


def conformer_conv(h, w_pw1, b_pw1, w_dw, b_dw, ln_g, ln_b, w_pw2, b_pw2):
    u = h @ w_pw1 + b_pw1
    a, gate = jnp.split(u, 2, axis=-1)
    u = a * jax.nn.sigmoid(gate)
    u = lax.conv_general_dilated(
        u, w_dw[:, None, :].astype(u.dtype), window_strides=(1,),
        padding=((CONV_WIDTH - 1, 0),),
        dimension_numbers=("NWC", "WIO", "NWC"),
        feature_group_count=D_MODEL) + b_dw
    u = jax.nn.silu(layer_norm(u, ln_g, ln_b))
    return u @ w_pw2 + b_pw2


def shared_kv(x, kv_norm, w_k, w_v, cos, sin):
    B, S, _ = x.shape
    h = rms_norm(x, kv_norm)
    k = apply_partial_rope((h @ w_k).reshape(B, S, DIFF_HEADS, 2, DIFF_HEAD_DIM), cos, sin)
    v = (h @ w_v).reshape(B, S, DIFF_HEADS, DIFF_V_DIM)
    return k, v


def diff_attention(h, k, v, cos, sin, w_q, lq1, lk1, lq2, lk2, subln_g, w_o, lambda_init):
    B, S, _ = h.shape
    q = apply_partial_rope((h @ w_q).reshape(B, S, DIFF_HEADS, 2, DIFF_HEAD_DIM), cos, sin)
    lam = (jnp.exp(jnp.sum(lq1.astype(jnp.float32) * lk1.astype(jnp.float32)))
           - jnp.exp(jnp.sum(lq2.astype(jnp.float32) * lk2.astype(jnp.float32)))
           + lambda_init)
    n_blocks = S // Q_BLOCK
    q_blocks = q.reshape(B, n_blocks, Q_BLOCK, DIFF_HEADS, 2, DIFF_HEAD_DIM).swapaxes(0, 1)
    k_pos = jnp.arange(S)
    scale = DIFF_HEAD_DIM ** -0.5

    def one_block(args):
        blk, qb = args
        q_pos = blk * Q_BLOCK + jnp.arange(Q_BLOCK)
        s = jnp.einsum('bqhcd,bkhcd->bhcqk', qb, k,
                       preferred_element_type=jnp.float32) * scale
        mask = k_pos[None, :] <= q_pos[:, None]
        p = jax.nn.softmax(jnp.where(mask, s, -jnp.inf), axis=-1)
        a = p[:, :, 0] - lam * p[:, :, 1]
        return jnp.einsum('bhqk,bkhe->bqhe', a.astype(v.dtype), v)

    o = lax.map(one_block, (jnp.arange(n_blocks), q_blocks))
    o = o.swapaxes(0, 1).reshape(B, S, DIFF_HEADS, DIFF_V_DIM)
    o = rms_norm(o, subln_g, SUBLN_EPS) * (1.0 - lambda_init)
    return o.reshape(B, S, DIFF_HEADS * DIFF_V_DIM) @ w_o


def memory_cross_attention(h, mem, w_q, w_k, w_v, w_o):
    B, S, _ = h.shape
    M = mem.shape[1]
    q = (h @ w_q).reshape(B, S, MEM_HEADS, MEM_HEAD_DIM)
    k = (mem @ w_k).reshape(B, M, MEM_HEADS, MEM_HEAD_DIM)
    v = (mem @ w_v).reshape(B, M, MEM_HEADS, MEM_HEAD_DIM)
    s = jnp.einsum('bshd,bmhd->bhsm', q, k, preferred_element_type=jnp.float32) * MEM_HEAD_DIM ** -0.5
    p = jax.nn.softmax(s, axis=-1)
    o = jnp.einsum('bhsm,bmhd->bshd', p.astype(v.dtype), v).reshape(B, S, D_MODEL)
    return o @ w_o


def swiglu(h, w_gate, w_up, w_down):
    return (jax.nn.silu(h @ w_gate) * (h @ w_up)) @ w_down


def setup_inputs(seed: int = 0) -> dict:
    key = jax.random.key(seed)
    ks = iter(jax.random.split(key, 40))
    f32 = jnp.float32

    def w(shape, fan_in):
        return jax.random.normal(next(ks), shape, f32) * (fan_in ** -0.5)

    def gain(shape):
        return 1.0 + 0.02 * jax.random.normal(next(ks), shape, f32)

    def small(shape, scale=0.01):
        return scale * jax.random.normal(next(ks), shape, f32)

    D, F, A, Bn = D_MODEL, FFN_HIDDEN, N_A_LAYERS, N_B_LAYERS
    QK = DIFF_HEADS * 2 * DIFF_HEAD_DIM
    VW = DIFF_HEADS * DIFF_V_DIM
    x = jax.random.normal(next(ks), (BATCH, SEQ, D), f32)
    mem = jax.random.normal(next(ks), (BATCH, MEM_TOKENS, D), f32)
    offset = jax.random.randint(next(ks), (BATCH, 1), 0, 1024, dtype=jnp.int32)
    positions = (offset + jnp.arange(SEQ, dtype=jnp.int32)[None, :]).astype(jnp.int32)
    return {
        "x": x, "mem": mem, "positions": positions,
        "norm_mix": gain((DEPTH, D)), "norm_mem": gain((DEPTH, D)),
        "norm_ffn": gain((DEPTH, D)), "norm_final": gain((D,)),
        "conv_w_pw1": w((A, D, 2 * D), D), "conv_b_pw1": small((A, 2 * D)),
        "conv_w_dw": w((A, CONV_WIDTH, D), CONV_WIDTH), "conv_b_dw": small((A, D)),
        "conv_ln_g": gain((A, D)), "conv_ln_b": small((A, D)),
        "conv_w_pw2": w((A, D, D), D), "conv_b_pw2": small((A, D)),
        "kv_norm": gain((D,)), "w_k_shared": w((D, QK), D), "w_v_shared": w((D, VW), D),
        "diff_w_q": w((Bn, D, QK), D),
        "diff_lambda_q1": small((Bn, DIFF_HEAD_DIM), 0.1), "diff_lambda_k1": small((Bn, DIFF_HEAD_DIM), 0.1),
        "diff_lambda_q2": small((Bn, DIFF_HEAD_DIM), 0.1), "diff_lambda_k2": small((Bn, DIFF_HEAD_DIM), 0.1),
        "diff_subln_g": gain((Bn, DIFF_V_DIM)), "diff_w_o": w((Bn, VW, D), VW),
        "mem_w_q": w((DEPTH, D, D), D), "mem_w_k": w((DEPTH, D, D), D),
        "mem_w_v": w((DEPTH, D, D), D), "mem_w_o": w((DEPTH, D, D), D),
        "ffn_w_gate": w((DEPTH, D, F), D), "ffn_w_up": w((DEPTH, D, F), D),
        "ffn_w_down": w((DEPTH, F, D), F),
    }


def reference(x, mem, positions, norm_mix, norm_mem, norm_ffn, norm_final,
              conv_w_pw1, conv_b_pw1, conv_w_dw, conv_b_dw, conv_ln_g, conv_ln_b,
              conv_w_pw2, conv_b_pw2, kv_norm, w_k_shared, w_v_shared,
              diff_w_q, diff_lambda_q1, diff_lambda_k1, diff_lambda_q2, diff_lambda_k2,
              diff_subln_g, diff_w_o, mem_w_q, mem_w_k, mem_w_v, mem_w_o,
              ffn_w_gate, ffn_w_up, ffn_w_down):
    cos, sin = rope_tables(positions)
    k_sh = None
    v_sh = None
    for i in range(DEPTH):
        if i < N_A_LAYERS:
            a = i
            h = rms_norm(x, norm_mix[i])
            x = x + conformer_conv(h, conv_w_pw1[a], conv_b_pw1[a], conv_w_dw[a], conv_b_dw[a],
                                   conv_ln_g[a], conv_ln_b[a], conv_w_pw2[a], conv_b_pw2[a])
        else:
            b = i - N_A_LAYERS
            if b == 0:
                k_sh, v_sh = shared_kv(x, kv_norm, w_k_shared, w_v_shared, cos, sin)
            lambda_init = 0.8 - 0.6 * math.exp(-0.3 * i)
            h = rms_norm(x, norm_mix[i])
            x = x + diff_attention(h, k_sh, v_sh, cos, sin, diff_w_q[b],
                                   diff_lambda_q1[b], diff_lambda_k1[b],
                                   diff_lambda_q2[b], diff_lambda_k2[b],
                                   diff_subln_g[b], diff_w_o[b], lambda_init)
        x = x + memory_cross_attention(rms_norm(x, norm_mem[i]), mem,
                                       mem_w_q[i], mem_w_k[i], mem_w_v[i], mem_w_o[i])
        x = x + swiglu(rms_norm(x, norm_ffn[i]), ffn_w_gate[i], ffn_w_up[i], ffn_w_down[i])
    return rms_norm(x, norm_final)
```

```python
import math
import numpy as np
import concourse.bass as bass
import concourse.mybir as mybir
from concourse.bass_utils import run_bass_kernel_spmd

F32 = mybir.dt.float32
BF16 = mybir.dt.bfloat16
I32 = mybir.dt.int32
AF = mybir.ActivationFunctionType
ALU = mybir.AluOpType
AX = mybir.AxisListType

D = 1024
KC = 8
SEQ = 4096
OWN = 2048
HALO = 64
TE = OWN + HALO
FF = 2816
FC = 22
DEPTH = 4
NA = 2
CONVW = 31
MEMT_N = 256
TILES = [(0, HALO)] + [(HALO + 512 * i, 512) for i in range(4)]
OWN_TILES = TILES[1:]
FFN_GROUPS = [(0, 4), (4, 4), (8, 4), (12, 4), (16, 4), (20, 2)]
RMS_EPS = 1e-6
LN_EPS = 1e-5
SUB_EPS = 1e-5
NEG_BIG = -30000.0
NO_CC = False
KV_PARTS = ("tables", "k", "v", "kstore", "vstore")
STRICT_SYNC = True


class Res:
    __slots__ = ("name", "last_w", "readers")

    def __init__(self, name):
        self.name = name
        self.last_w = None
        self.readers = []


class Op:
    __slots__ = ("eng", "fn", "deps", "kind", "needs_signal", "sig", "tag")

    def __init__(self, eng, fn, kind, tag):
        self.eng = eng
        self.fn = fn
        self.deps = []
        self.kind = kind
        self.needs_signal = False
        self.sig = None
        self.tag = tag


class Sched:
    ENGS = ("pe", "act", "dve", "pool", "sp")
    EPOCH = 30000

    def __init__(self, nc, n_dma_sems=20):
        self.nc = nc
        self.ops = []
        self.e = {"pe": nc.tensor, "act": nc.scalar, "dve": nc.vector,
                  "pool": nc.gpsimd, "sp": nc.sync}
        self.n_dma_sems = n_dma_sems

    def last_op(self, eng):
        for op in reversed(self.ops):
            if op.eng == eng and op.kind == "c":
                return op
        return None

    def add(self, eng, fn, reads=(), writes=(), kind="c", tag="", after=()):
        op = Op(eng, fn, kind, tag)
        deps = {}
        for d in after:
            if d is not None:
                deps.setdefault(id(d), [d, True])
        for r in reads:
            if r.last_w is not None:
                deps.setdefault(id(r.last_w), [r.last_w, False])[1] = True
        for w in writes:
            if w.last_w is not None:
                deps.setdefault(id(w.last_w), [w.last_w, False])
            for rd in w.readers:
                deps.setdefault(id(rd), [rd, False])
        for d, raw in deps.values():
            if d is op:
                continue
            if d.eng == eng and d.kind == "c" and kind == "c":
                if eng == "pe" or (not raw and not STRICT_SYNC):
                    continue
            d.needs_signal = True
            op.deps.append(d)
        for r in reads:
            if kind == "c":
                r.readers = [x for x in r.readers if not (x.kind == "c" and x.eng == eng)]
            r.readers.append(op)
        for w in writes:
            w.last_w = op
            w.readers = []
        self.ops.append(op)
        return op

    def emit(self, final_wait_ops=()):
        nc = self.nc
        n_sig = {k: 0 for k in self.ENGS}
        for op in self.ops:
            if op.kind == "c" and op.needs_signal:
                n_sig[op.eng] += 1
        sems = {k: [nc.alloc_semaphore(f"s_{k}{j}") for j in range(n_sig[k] // self.EPOCH + 1)]
                for k in self.ENGS}
        nds = self.n_dma_sems
        dsems = {q: [nc.alloc_semaphore(f"s_dma_{q}{i}") for i in range(nds)] for q in ("sp", "pool")}
        cnt = {k: 0 for k in self.ENGS}
        dcnt = {q: [0] * nds for q in dsems}
        dlast = {q: [None] * nds for q in dsems}
        ndq = {q: 0 for q in dsems}
        known = {k: {} for k in self.ENGS}
        nd = 0
        nwaits = 0
        ncc = 0

        def wait(engname, sig):
            nonlocal nwaits
            sem, val, key = sig
            if known[engname].get(key, 0) >= val:
                return
            self.e[engname].wait_ge(sem, val)
            known[engname][key] = val
            nwaits += 1

        for op in self.ops:
            E = self.e[op.eng]
            for d in op.deps:
                assert d.sig is not None, (op.tag, d.tag)
                wait(op.eng, d.sig)
            if op.kind == "d":
                q = op.eng
                i = ndq[q] % nds
                ndq[q] += 1
                nd += 1
                if dlast[q][i] is not None:
                    wait(op.eng, dlast[q][i])
                ins = op.fn(E)
                dcnt[q][i] += 16
                ins.then_inc(dsems[q][i], 16)
                op.sig = (dsems[q][i], dcnt[q][i], ("d", q, i))
                dlast[q][i] = op.sig
            elif op.kind == "cc":
                s = nc.alloc_semaphore(f"s_cc{ncc}")
                ncc += 1
                ins = op.fn(E)
                ins.then_inc(s, 1)
                op.sig = (s, 1, ("cc", ncc))
            else:
                ins = op.fn(E)
                if op.needs_signal:
                    ep, c = divmod(cnt[op.eng], self.EPOCH)
                    cnt[op.eng] += 1
                    ins.then_inc(sems[op.eng][ep], 1)
                    op.sig = (sems[op.eng][ep], c + 1, (op.eng, ep))
        for op in final_wait_ops:
            wait("sp", op.sig)
        self.stats = dict(n_ops=len(self.ops), n_waits=nwaits, sig=dict(cnt), n_dma=nd)
        return self.stats


def _vec_layout():
    cols = {}
    cur = 0

    def addv(name, n):
        nonlocal cur
        cols[name] = cur
        cur += n

    for i in range(DEPTH):
        addv(f"norm_mix{i}", 8)
        addv(f"norm_mem{i}", 8)
        addv(f"norm_ffn{i}", 8)
    addv("norm_final", 8)
    addv("kv_norm", 8)
    for a in range(NA):
        addv(f"b_pw1{a}", 16)
        addv(f"b_dw{a}", 8)
        addv(f"ln_g{a}", 8)
        addv(f"ln_b{a}", 8)
        addv(f"b_pw2{a}", 8)
        addv(f"wdw{a}", CONVW * 8)
    for b in range(2):
        addv(f"subg{b}", 1)
    for nm in ("hv", "rectb", "freq", "eps_rms", "eps_ln", "eps_sub", "zero"):
        addv(nm, 1)
    return cols, cur


VCOL, NV = _vec_layout()


def _wl(w):
    k, n = w.shape
    return np.ascontiguousarray(w.reshape(k // 128, 128, n).transpose(1, 0, 2))


def _col(v):
    return v.reshape(-1, 128).T


def build_program(n_cores=8, stop_after=None, mode="fused"):
    nc = bass.Bass("TRN2", target_bir_lowering=False)
    S = Sched(nc)

    def dram_in(name, shape, dt=F32):
        return nc.dram_tensor(name, list(shape), dt, kind="ExternalInput")

    xT_d = dram_in("xT", [128, KC, TE])
    memT_d = dram_in("memT", [128, KC, MEMT_N])
    pos_d = dram_in("pos", [1, OWN], I32)
    vecs_d = dram_in("vecs", [128, NV])
    lamin_d = dram_in("lamin", [1, 512])
    cst_d = dram_in("cst", [128, 512])
    WSHAPE = {}
    for a in range(NA):
        WSHAPE[f"pw1{a}"] = [128, KC, 2 * D]
        WSHAPE[f"pw2{a}"] = [128, KC, D]
    WSHAPE["ksh"] = [128, KC, D]
    WSHAPE["vsh"] = [128, KC, D]
    for b in range(2):
        WSHAPE[f"dq{b}"] = [128, KC, D]
        WSHAPE[f"do{b}"] = [128, KC, D]
    for i in range(DEPTH):
        for nm in ("mq", "mk", "mv", "mo"):
            WSHAPE[f"{nm}{i}"] = [128, KC, D]
        WSHAPE[f"fg{i}"] = [128, KC, FF]
        WSHAPE[f"fu{i}"] = [128, KC, FF]
        WSHAPE[f"fd{i}"] = [128, FC, D]

    class _LazyW(dict):
        def __missing__(self, key):
            self[key] = dram_in("w_" + key, WSHAPE[key])
            return self[key]

    W = _LazyW()
    if stop_after is None and mode == "fused":
        for key in WSHAPE:
            W[key]
    y_d = nc.dram_tensor("y", [128, KC, OWN], F32, kind="ExternalOutput")
    if mode == "B":
        kv_mine = [nc.dram_tensor(f"kvm{p}", [128, OWN], BF16, kind="ExternalInput") for p in range(16)]
        kv_gath = [nc.dram_tensor(f"kvg{p}", [256, OWN], BF16, kind="ExternalInput") for p in range(16)]
    else:
        kv_mine = [nc.dram_tensor(f"kvm{p}", [128, OWN], BF16, kind=("ExternalOutput" if mode == "A" else "Internal")) for p in range(16)]
        kv_gath = [nc.dram_tensor(f"kvg{p}", [256, OWN], BF16, kind="Internal") for p in range(16)] if mode == "fused" else None
    R_kvmine = [Res(f"kvmine{p}") for p in range(16)]
    R_kvgath = [Res(f"kvgath{p}") for p in range(16)]

    arena = nc.alloc_sbuf_tensor("arena", [128, 212480], mybir.dt.uint8)
    base = nc.lookup_mloc(arena).addr
    cur = [base]

    def at(name, shape, dt, off=None, advance=True):
        nbytes = int(np.prod(shape[1:])) * (4 if dt in (F32, I32) else 2)
        o = cur[0] if off is None else off
        t = nc.alloc_sbuf_tensor_at(name, list(shape), dt, offset=o)
        if off is None and advance:
            cur[0] += (nbytes + 31) // 32 * 32
        return t

    X = at("X", [128, KC, TE], F32)
    H = at("H", [128, KC, TE], BF16)
    VEC = at("VEC", [128, NV], F32)
    CONST = at("CONST", [128, 512], BF16)
    NEGH = at("NEGH", [128, 512], BF16)
    LAMV = at("LAMV", [128, 16], F32)
    RING_OFF = cur[0]
    NSLOT = 5
    cur[0] += NSLOT * 8192
    PL = cur[0]
    PL_END = base + 212480
    CS_OFF = PL_END - 8192
    assert PL + 56576 <= CS_OFF, (PL, CS_OFF)

    RX = [[Res(f"X{k}_{t}") for t in range(5)] for k in range(KC)]
    RH = [[Res(f"H{k}_{t}") for t in range(5)] for k in range(KC)]
    R_vec = Res("vec")
    R_const = Res("const")
    R_negh = Res("negh")
    R_lamv = Res("lamv")
    R_slot = [Res(f"slot{i}") for i in range(NSLOT)]
    slot_flat = [at(f"slot{i}", [128, 4096], BF16, off=RING_OFF + i * 8192) for i in range(NSLOT)]
    slot_k = [at(f"slotk{i}", [128, 8, 512], BF16, off=RING_OFF + i * 8192) for i in range(NSLOT)]
    slot_d = [at(f"slotd{i}", [128, 4, 1024], BF16, off=RING_OFF + i * 8192) for i in range(NSLOT)]
    ring_i = [0]

    def ring_next():
        i = ring_i[0] % NSLOT
        ring_i[0] += 1
        return i

    ident = CONST[:, 0:128]
    ones = CONST[:, 128:256]
    PTm = CONST[:, 256:384]
    maskd = CONST[:, 384:512]

    def vcol(name, j=0):
        c = VCOL[name] + j
        return VEC[:, c:c + 1]

    SQ = [at(f"SQ{i}", [128, 512], BF16, off=PL + i * 1024) for i in range(2)]
    R_sq = [Res(f"sq{i}") for i in range(2)]
    ST = [at(f"ST{i}", [128, 512], F32, off=PL + 2048 + i * 2048) for i in range(4)]
    R_st = [Res(f"st{i}") for i in range(4)]
    PLP = PL + 10240
    sq_i = [0]
    st_i = [0]

    def sq_next():
        i = sq_i[0] % 2
        sq_i[0] += 1
        return SQ[i], R_sq[i]

    def st_next():
        i = st_i[0] % 4
        st_i[0] += 1
        return ST[i], R_st[i]

    PS = [nc.alloc_psum_tensor(f"ps{i}", [128, 512], F32) for i in range(8)]
    R_ps = [Res(f"ps{i}") for i in range(8)]
    ps_i = [0]

    pinned = set()

    def bank(pin=False):
        while True:
            i = ps_i[0] % 8
            ps_i[0] += 1
            if i not in pinned:
                break
        if pin:
            pinned.add(i)
        return PS[i], R_ps[i]

    def unpin_all():
        pinned.clear()

    def dma(eng, out, in_, reads=(), writes=(), tag="dma", after=()):
        return S.add(eng, lambda e: e.dma_start(out=out, in_=in_), reads, writes, kind="d", tag=tag, after=after)

    def mm(out, lhsT, rhs, start, stop, reads, writes):
        return S.add("pe", lambda e: e.matmul(out, lhsT, rhs, start=start, stop=stop), reads, writes, tag="mm")

    def act(out, in_, func, reads, writes, bias=None, scale=1.0):
        if bias is None:
            return S.add("act", lambda e: e.activation(out=out, in_=in_, func=func, scale=scale), reads, writes, tag="act")
        return S.add("act", lambda e: e.activation(out=out, in_=in_, func=func, bias=bias, scale=scale), reads, writes, tag="act")

    def tt(out, in0, in1, op, reads, writes, eng="dve"):
        return S.add(eng, lambda e: e.tensor_tensor(out=out, in0=in0, in1=in1, op=op), reads, writes, tag="tt")

    def ts(out, in0, s1, s2, op0, op1, reads, writes, eng="dve"):
        if op1 is None:
            return S.add(eng, lambda e: e.tensor_scalar(out=out, in0=in0, scalar1=s1, scalar2=None, op0=op0), reads, writes, tag="ts")
        return S.add(eng, lambda e: e.tensor_scalar(out=out, in0=in0, scalar1=s1, scalar2=s2, op0=op0, op1=op1), reads, writes, tag="ts")

    def stt(out, in0, scalar, in1, op0, op1, reads, writes, eng="dve"):
        return S.add(eng, lambda e: e.scalar_tensor_tensor(out=out, in0=in0, scalar=scalar, in1=in1, op0=op0, op1=op1), reads, writes, tag="stt")

    def copy(out, in_, reads, writes, eng="dve"):
        if eng == "act":
            return S.add("act", lambda e: e.activation(out=out, in_=in_, func=AF.Copy), reads, writes, tag="copy")
        return S.add(eng, lambda e: e.tensor_copy(out=out, in_=in_), reads, writes, tag="copy")

    def memset(ap, val, writes, eng="dve", after=()):
        return S.add(eng, lambda e: e.memset(ap, val), (), writes, tag="memset", after=after)

    def powm05(ap, w, res):
        return S.add("pool", lambda e: e.tensor_tensor(out=ap, in0=ap, in1=NEGH[:, 0:w], op=ALU.pow), [res, R_negh], [res], tag="pow")

    def recip(out, in_, reads, writes):
        return S.add("dve", lambda e: e.reciprocal(out=out, in_=in_), reads, writes, tag="recip")

    def load_w(wd, kind, a0, a1):
        i = ring_next()
        if kind == "k":
            dma("pool", slot_k[i][:, :, 0:a1 - a0], wd.ap()[:, :, a0:a1], (), [R_slot[i]], tag="wload")
            return slot_k[i], R_slot[i]
        dma("pool", slot_d[i][:, 0:a1 - a0, :], wd.ap()[:, a0:a1, :], (), [R_slot[i]], tag="wload")
        return slot_d[i], R_slot[i]

    def rstd_from_sumsq(ps, rps, w, n, eps_name, extra_reads=()):
        st, rst = st_next()
        act(st[:, 0:w], ps[:, 0:w], AF.Sqrt, [rps, R_vec], [rst], bias=vcol(eps_name), scale=1.0 / n)
        recip(st[:, 0:w], st[:, 0:w], [rst], [rst])
        return st, rst

    def rmsnorm_tile(t, gname, src_own_only=False):
        c0, w = TILES[t]
        ps, rps = bank()
        for k in range(KC):
            sq, rsq = sq_next()
            act(sq[:, 0:w], X[:, k, c0:c0 + w], AF.Square, [RX[k][t]], [rsq])
            mm(ps[:, 0:w], ones, sq[:, 0:w], k == 0, k == KC - 1, [rsq, R_const], [rps])
        st, rst = rstd_from_sumsq(ps, rps, w, D, "eps_rms")
        for k in range(KC):
            stt(H[:, k, c0:c0 + w], X[:, k, c0:c0 + w], vcol(gname, k), st[:, 0:w], ALU.mult, ALU.mult,
                [RX[k][t], rst, R_vec], [RH[k][t]])

    def linear_to_x(wname, tiles, src, rsrc, bias_name=None):
        for s in range(2):
            wt, rw = load_w(W[wname], "k", s * 512, s * 512 + 512)
            for t in tiles:
                c0, w = TILES[t]
                for mm_ in range(4):
                    m = s * 4 + mm_
                    ps, rps = bank()
                    for k in range(KC):
                        mm(ps[:, 0:w], wt[:, k, mm_ * 128:(mm_ + 1) * 128], src[:, k, c0:c0 + w],
                           k == 0, k == KC - 1, [rw, rsrc[k][t]], [rps])
                    if bias_name is None:
                        tt(X[:, m, c0:c0 + w], ps[:, 0:w], X[:, m, c0:c0 + w], ALU.add, [rps, RX[m][t]], [RX[m][t]])
                    else:
                        stt(X[:, m, c0:c0 + w], ps[:, 0:w], vcol(bias_name, m), X[:, m, c0:c0 + w], ALU.add, ALU.add,
                            [rps, RX[m][t], R_vec], [RX[m][t]])

    for k in range(KC):
        dma("sp", X[:, k, :], xT_d.ap()[:, k, :], (), [RX[k][t] for t in range(5)], tag="xload")
    dma("sp", VEC[:, :], vecs_d.ap(), (), [R_vec])
    dma("pool", CONST[:, :], cst_d.ap(), (), [R_const])

    LAMT = at("LAMT", [128, 512], F32, off=PLP)
    R_lamt = Res("lamt")
    dma("sp", LAMT[:, :], bass.AP(lamin_d, 0, [[0, 128], [1, 512]]), (), [R_lamt])
    LSC = at("LSC", [128, 64], F32, off=PLP + 2048)
    R_lsc = Res("lsc")
    LRED = at("LRED", [128, 8], F32, off=PLP + 2048 + 256)
    R_lred = Res("lred")
    for b in range(2):
        for j in range(2):
            o = b * 256 + j * 128
            tt(LSC[:, :], LAMT[:, o:o + 64], LAMT[:, o + 64:o + 128], ALU.mult, [R_lamt], [R_lsc])
            S.add("dve", lambda e, oo=LRED[:, 2 * b + j:2 * b + j + 1]: e.tensor_reduce(out=oo, in_=LSC[:, :], axis=AX.X, op=ALU.add),
                  [R_lsc], [R_lred], tag="lred")
    act(LRED[:, 4:8], LRED[:, 0:4], AF.Exp, [R_lred], [R_lred])
    for b in range(2):
        li = 0.8 - 0.6 * math.exp(-0.3 * (NA + b))
        tt(LAMV[:, 2 * b:2 * b + 1], LRED[:, 4 + 2 * b:5 + 2 * b], LRED[:, 5 + 2 * b:6 + 2 * b], ALU.subtract, [R_lred], [R_lamv])
        ts(LAMV[:, 2 * b:2 * b + 1], LAMV[:, 2 * b:2 * b + 1], li, None, ALU.add, None, [R_lamv], [R_lamv])
        ts(LAMV[:, 2 * b + 1:2 * b + 2], LAMV[:, 2 * b:2 * b + 1], -1.0, None, ALU.mult, None, [R_lamv], [R_lamv])
        ts(LAMV[:, 4 + b:5 + b], vcol(f"subg{b}"), 1.0 - li, None, ALU.mult, None, [R_vec, R_lamv], [R_lamv])

    final_ops = []
    kvstores = []

    def dump_and_finish():
        if mode == "A":
            final_ops.extend(kvstores)
        for k in range(KC):
            final_ops.append(dma("sp", y_d.ap()[:, k, :], X[:, k, HALO:TE], [RX[k][t] for t in range(1, 5)], ()))

    def conv_mixer(i):
        a = i
        tiles = list(range(5))
        fence = S.last_op("pe")
        for t in tiles:
            rmsnorm_tile(t, f"norm_mix{i}")
        U = at(f"U{i}", [128, KC, 32 + TE], BF16, off=PLP)
        RU = [[Res(f"U{k}_{t}") for t in range(5)] for k in range(KC)]
        RUpad = [Res(f"Upad{k}") for k in range(KC)]
        DIAG = at(f"DIAG{i}", [128, CONVW, 128], BF16, off=PLP + 34304)
        R_diag = [Res(f"diag{j}") for j in range(CONVW)]
        GATE = [at(f"GATE{i}_{j}", [128, 512], F32, off=PLP + 34304 + 7936) for j in range(2)]
        R_gate = [Res("gate0")] * 2
        T1 = [at(f"T1{i}_{j}", [128, 512], F32, off=PLP + 34304 + 7936 + 2048) for j in range(2)]
        R_t1 = [Res("t10")] * 2
        for k in range(KC):
            S.add("dve", lambda e, ap=U[:, k, 0:32]: e.memset(ap, 0.0), [RH[0][t] for t in tiles], [RUpad[k]],
                  tag="padzero", after=[fence])
        gi = 0
        for g in range(2):
            wa, rwa = load_w(W[f"pw1{a}"], "k", g * 512, g * 512 + 512)
            wg, rwg = load_w(W[f"pw1{a}"], "k", D + g * 512, D + g * 512 + 512)
            for t in tiles:
                c0, w = TILES[t]
                for mm_ in range(4):
                    m = g * 4 + mm_
                    pa, rpa = bank()
                    for k in range(KC):
                        mm(pa[:, 0:w], wa[:, k, mm_ * 128:(mm_ + 1) * 128], H[:, k, c0:c0 + w], k == 0, k == KC - 1,
                           [rwa, RH[k][t]], [rpa])
                    pg, rpg = bank()
                    for k in range(KC):
                        mm(pg[:, 0:w], wg[:, k, mm_ * 128:(mm_ + 1) * 128], H[:, k, c0:c0 + w], k == 0, k == KC - 1,
                           [rwg, RH[k][t]], [rpg])
                    gt, rgt = GATE[gi % 2], R_gate[gi % 2]
                    gi += 1
                    act(gt[:, 0:w], pg[:, 0:w], AF.Sigmoid, [rpg, R_vec], [rgt], bias=vcol(f"b_pw1{a}", 8 + m))
                    stt(U[:, m, 32 + c0:32 + c0 + w], pa[:, 0:w], vcol(f"b_pw1{a}", m), gt[:, 0:w], ALU.add, ALU.mult,
                        [rpa, rgt, R_vec], [RU[m][t]])
                    if t == 0:
                        ts(U[:, m, 32:32 + HALO], U[:, m, 32:32 + HALO], vcol("hv"), None, ALU.mult, None,
                           [RU[m][0], R_vec], [RU[m][0]])
        for k in range(KC):
            for j in range(CONVW):
                ts(DIAG[:, j, :], ident, vcol(f"wdw{a}", j * 8 + k), None, ALU.mult, None, [R_const, R_vec], [R_diag[j]])
            for t in tiles:
                c0, w = TILES[t]
                ps, rps = bank()
                rds = [RU[k][t], RUpad[k]] + ([RU[k][t - 1]] if t > 0 else [])
                for j in range(CONVW):
                    o = c0 + 2 + j
                    mm(ps[:, 0:w], DIAG[:, j, :], U[:, k, o:o + w], j == 0, j == CONVW - 1, [R_diag[j]] + rds, [rps])
                act(H[:, k, c0:c0 + w], ps[:, 0:w], AF.Identity, [rps, R_vec], [RH[k][t]], bias=vcol(f"b_dw{a}", k))
        ti = 0
        for t in tiles:
            c0, w = TILES[t]
            p1, rp1 = bank()
            p2, rp2 = bank()
            for k in range(KC):
                sq, rsq = sq_next()
                act(sq[:, 0:w], H[:, k, c0:c0 + w], AF.Square, [RH[k][t]], [rsq])
                mm(p1[:, 0:w], ones, H[:, k, c0:c0 + w], k == 0, k == KC - 1, [RH[k][t], R_const], [rp1])
                mm(p2[:, 0:w], ones, sq[:, 0:w], k == 0, k == KC - 1, [rsq, R_const], [rp2])
            mean, rmean = st_next()
            ts(mean[:, 0:w], p1[:, 0:w], 1.0 / D, None, ALU.mult, None, [rp1], [rmean])
            msq, rmsq = st_next()
            tt(msq[:, 0:w], mean[:, 0:w], mean[:, 0:w], ALU.mult, [rmean], [rmsq])
            var, rvar = st_next()
            stt(var[:, 0:w], p2[:, 0:w], 1.0 / D, msq[:, 0:w], ALU.mult, ALU.subtract, [rp2, rmsq], [rvar])
            act(var[:, 0:w], var[:, 0:w], AF.Sqrt, [rvar, R_vec], [rvar], bias=vcol("eps_ln"))
            recip(var[:, 0:w], var[:, 0:w], [rvar], [rvar])
            for k in range(KC):
                t1, rt1 = T1[ti % 2], R_t1[ti % 2]
                ti += 1
                tt(t1[:, 0:w], H[:, k, c0:c0 + w], mean[:, 0:w], ALU.subtract, [RH[k][t], rmean], [rt1])
                tt(t1[:, 0:w], t1[:, 0:w], var[:, 0:w], ALU.mult, [rt1, rvar], [rt1])
                act(H[:, k, c0:c0 + w], t1[:, 0:w], AF.Silu, [rt1, R_vec], [RH[k][t]],
                    bias=vcol(f"ln_b{a}", k), scale=vcol(f"ln_g{a}", k))
        linear_to_x(f"pw2{a}", tiles, H, RH, bias_name=f"b_pw2{a}")

    def mem_attn(i, tiles):
        QT = [at(f"QT{i}_{j}", [128, KC, 512], BF16, off=PLP + j * 8192) for j in range(2)]
        R_qt = [[Res(f"qt{j}_{k}") for k in range(KC)] for j in range(2)]
        KMT = at(f"KMT{i}", [128, KC, MEMT_N], BF16, off=PLP + 16384)
        R_kmt = [Res(f"kmt{k}") for k in range(KC)]
        VM = at(f"VM{i}", [128, 2, D], BF16, off=PLP + 20480)
        R_vm = [[Res(f"vm{mt}_{s}") for s in range(2)] for mt in range(2)]
        ET = [at(f"ET{i}_{j}", [128, 2, 512], BF16, off=PLP + 24576 + j * 2048) for j in range(2)]
        R_et = [[Res(f"et{j}_{mh}") for mh in range(2)] for j in range(2)]
        MEMT = at(f"MEMT{i}", [128, KC, MEMT_N], BF16, off=PLP + 28672)
        R_memt = Res("memt")
        dma("pool", MEMT[:, :, :], memT_d.ap(), (), [R_memt], after=[S.last_op("pe")])
        for s in range(2):
            wt, rw = load_w(W[f"mk{i}"], "k", s * 512, s * 512 + 512)
            for mm_ in range(4):
                m = s * 4 + mm_
                ps, rps = bank()
                for k in range(KC):
                    mm(ps[:, 0:MEMT_N], wt[:, k, mm_ * 128:(mm_ + 1) * 128], MEMT[:, k, :], k == 0, k == KC - 1,
                       [rw, R_memt], [rps])
                copy(KMT[:, m, :], ps[:, 0:MEMT_N], [rps], [R_kmt[m]])
        for s in range(2):
            wt, rw = load_w(W[f"mv{i}"], "k", s * 512, s * 512 + 512)
            for mt in range(2):
                ps, rps = bank()
                for k in range(KC):
                    mm(ps[:, :], MEMT[:, k, mt * 128:(mt + 1) * 128], wt[:, k, :], k == 0, k == KC - 1,
                       [rw, R_memt], [rps])
                copy(VM[:, mt, s * 512:(s + 1) * 512], ps[:, :], [rps], [R_vm[mt][s]])
        wq = [load_w(W[f"mq{i}"], "k", s * 512, s * 512 + 512) for s in range(2)]
        ei = 0
        for ti, t in enumerate(tiles):
            c0, w = TILES[t]
            rmsnorm_tile(t, f"norm_mem{i}")
            qt, rqt = QT[ti % 2], R_qt[ti % 2]
            for m in range(KC):
                wt, rw = wq[m // 4]
                ps, rps = bank()
                for k in range(KC):
                    mm(ps[:, 0:w], wt[:, k, (m % 4) * 128:(m % 4 + 1) * 128], H[:, k, c0:c0 + w], k == 0, k == KC - 1,
                       [rw, RH[k][t]], [rps])
                act(qt[:, m, 0:w], ps[:, 0:w], AF.Copy, [rps], [rqt[m]])
            for hh in range(4):
                et, ret = ET[ei % 2], R_et[ei % 2]
                ei += 1
                for mh in range(2):
                    ps, rps = bank()
                    for dd in range(2):
                        c = 2 * hh + dd
                        mm(ps[:, 0:w], KMT[:, c, mh * 128:(mh + 1) * 128], qt[:, c, 0:w], dd == 0, dd == 1,
                           [R_kmt[c], rqt[c]], [rps])
                    act(et[:, mh, 0:w], ps[:, 0:w], AF.Exp, [rps], [ret[mh]], scale=1.0 / 16.0)
                pd, rpd = bank()
                for mh in range(2):
                    mm(pd[:, 0:w], ones, et[:, mh, 0:w], mh == 0, mh == 1, [R_const, ret[mh]], [rpd])
                rd, rrd = st_next()
                recip(rd[:, 0:w], pd[:, 0:w], [rpd], [rrd])
                for dd in range(2):
                    c = 2 * hh + dd
                    po, rpo = bank()
                    for mh in range(2):
                        mm(po[:, 0:w], VM[:, mh, c * 128:(c + 1) * 128], et[:, mh, 0:w], mh == 0, mh == 1,
                           [R_vm[mh][c // 4], ret[mh]], [rpo])
                    tt(H[:, c, c0:c0 + w], po[:, 0:w], rd[:, 0:w], ALU.mult, [rpo, rrd], [RH[c][t]])
        linear_to_x(f"mo{i}", tiles, H, RH)

    def ffn(i, tiles):
        HID = [at(f"HID{i}_{j}", [128, 4, 512], BF16, off=PLP + j * 4096) for j in range(2)]
        R_hid = [[Res(f"hid{j}_{q}") for q in range(4)] for j in range(2)]
        SG = [at(f"SG{i}_{j}", [128, 512], F32, off=PLP + 8192 + j * 2048) for j in range(2)]
        R_sg = [Res(f"sg{j}") for j in range(2)]
        for t in tiles:
            rmsnorm_tile(t, f"norm_ffn{i}")
        hi = 0
        si = 0
        for (j0, gs) in FFN_GROUPS:
            wg, rwg = load_w(W[f"fg{i}"], "k", j0 * 128, (j0 + gs) * 128)
            wu, rwu = load_w(W[f"fu{i}"], "k", j0 * 128, (j0 + gs) * 128)
            wd, rwd = load_w(W[f"fd{i}"], "d", j0, j0 + gs)
            for t in tiles:
                c0, w = TILES[t]
                hid, rhid = HID[hi % 2], R_hid[hi % 2]
                hi += 1
                for jj in range(gs):
                    pg, rpg = bank()
                    for k in range(KC):
                        mm(pg[:, 0:w], wg[:, k, jj * 128:(jj + 1) * 128], H[:, k, c0:c0 + w], k == 0, k == KC - 1,
                           [rwg, RH[k][t]], [rpg])
                    pu, rpu = bank()
                    for k in range(KC):
                        mm(pu[:, 0:w], wu[:, k, jj * 128:(jj + 1) * 128], H[:, k, c0:c0 + w], k == 0, k == KC - 1,
                           [rwu, RH[k][t]], [rpu])
                    sg, rsg = SG[si % 2], R_sg[si % 2]
                    si += 1
                    act(sg[:, 0:w], pg[:, 0:w], AF.Silu, [rpg], [rsg])
                    tt(hid[:, jj, 0:w], sg[:, 0:w], pu[:, 0:w], ALU.mult, [rsg, rpu], [rhid[jj]])
                for m in range(KC):
                    ps, rps = bank()
                    for jj in range(gs):
                        mm(ps[:, 0:w], wd[:, jj, m * 128:(m + 1) * 128], hid[:, jj, 0:w], jj == 0, jj == gs - 1,
                           [rwd, rhid[jj]], [rps])
                    tt(X[:, m, c0:c0 + w], ps[:, 0:w], X[:, m, c0:c0 + w], ALU.add, [rps, RX[m][t]], [RX[m][t]])

    CS = at("CS", [128, 2, OWN], BF16, off=CS_OFF)
    R_cs = [Res(f"cs{t}") for t in range(4)]

    def rope_tables():
        PI = at("PI", [128, 512], I32, off=PLP + 16384)
        PF = at("PF", [128, 512], F32, off=PLP + 16384 + 2048)
        TF = at("TF", [128, 512], F32, off=PLP + 16384 + 4096)
        FR = at("FR", [128, 512], F32, off=PLP + 16384 + 6144)
        R_pi, R_pf, R_tf, R_fr = Res("pi"), Res("pf"), Res("tf"), Res("fr")
        fence = S.last_op("pe")
        for t in range(4):
            dma("sp", PI[:, :], bass.AP(pos_d, t * 512, [[0, 128], [1, 512]]), (), [R_pi], after=[fence])
            copy(PF[:, :], PI[:, :], [R_pi], [R_pf])
            ts(PF[:, :], PF[:, :], vcol("freq"), 1.0 / (2.0 * math.pi), ALU.mult, ALU.mult, [R_pf, R_vec], [R_pf])
            for which in (1, 0):
                if which == 0:
                    ts(PF[:, :], PF[:, :], 0.25, None, ALU.add, None, [R_pf], [R_pf])
                copy(PI[:, :], PF[:, :], [R_pf], [R_pi])
                copy(TF[:, :], PI[:, :], [R_pi], [R_tf])
                tt(FR[:, :], PF[:, :], TF[:, :], ALU.subtract, [R_pf, R_tf], [R_fr])
                act(CS[:, which, t * 512:(t + 1) * 512], FR[:, :], AF.Sin, [R_fr], [R_cs[t]], scale=2.0 * math.pi)

    rope_tables()

    KBQ = [at(f"KBQ{j}", [128, 512], BF16, off=PLP + 8192 + j * 1024) for j in range(2)]
    R_kbq = [Res(f"kbq{j}") for j in range(2)]
    rope_i = [0]

    def rope_from_psum(ps, rps, to, out_ap, out_res):
        j = rope_i[0] % 2
        rope_i[0] += 1
        kb, rkb = KBQ[j], R_kbq[j]
        t1, rt1 = st_next()
        act(kb[:, :], ps[:, :], AF.Copy, [rps], [rkb])
        p2, rp2 = bank()
        mm(p2[:, :], PTm, kb[:, :], True, True, [R_const, rkb], [rp2])
        tt(t1[:, :], ps[:, :], CS[:, 0, to * 512:(to + 1) * 512], ALU.mult, [rps, R_cs[to], rkb], [rt1])
        st, rst = st_next()
        tt(st[:, :], p2[:, :], CS[:, 1, to * 512:(to + 1) * 512], ALU.mult, [rp2, R_cs[to]], [rst])
        tt(out_ap, t1[:, :], st[:, :], ALU.add, [rt1, rst] + R_cs, out_res)

    def shared_kv():
        for t in range(1, 5):
            rmsnorm_tile(t, "kv_norm")
        KST = [at(f"KST{j}", [128, 512], BF16, off=PLP + j * 1024) for j in range(2)]
        R_kst = [Res(f"kst{j}") for j in range(2)]
        VST = [at(f"VST{j}", [128, 512], BF16, off=PLP + 2048 + j * 1024) for j in range(2)]
        R_vst = [Res(f"vst{j}") for j in range(2)]
        ki = 0
        for s in range(2 if "k" in KV_PARTS else 0):
            wt, rw = load_w(W["ksh"], "k", s * 512, s * 512 + 512)
            for to in range(4):
                t = to + 1
                c0, w = TILES[t]
                for mm_ in range(4):
                    h = s * 4 + mm_
                    ps, rps = bank()
                    for k in range(KC):
                        mm(ps[:, :], wt[:, k, mm_ * 128:(mm_ + 1) * 128], H[:, k, c0:c0 + w], k == 0, k == KC - 1,
                           [rw, RH[k][t]], [rps])
                    kst, rkst = KST[ki % 2], R_kst[ki % 2]
                    ki += 1
                    rope_from_psum(ps, rps, to, kst[:, :], [rkst])
                    if "kstore" in KV_PARTS:
                        kvstores.append(dma("sp", kv_mine[h].ap()[:, to * 512:(to + 1) * 512], kst[:, :], [rkst], [R_kvmine[h]]))
        vi = 0
        for s in range(2 if "v" in KV_PARTS else 0):
            wt, rw = load_w(W["vsh"], "k", s * 512, s * 512 + 512)
            for to in range(4):
                t = to + 1
                c0, w = TILES[t]
                for sub in range(4):
                    blk = to * 4 + sub
                    ps, rps = bank()
                    for k in range(KC):
                        mm(ps[:, :], H[:, k, c0 + sub * 128:c0 + (sub + 1) * 128], wt[:, k, :], k == 0, k == KC - 1,
                           [rw, RH[k][t]], [rps])
                    vst, rvst = VST[vi % 2], R_vst[vi % 2]
                    vi += 1
                    copy(vst[:, :], ps[:, :], [rps], [rvst])
                    for hh in range(4 if "vstore" in KV_PARTS else 0):
                        h = s * 4 + hh
                        kvstores.append(dma("sp", kv_mine[8 + h].ap()[:, blk * 128:(blk + 1) * 128],
                                            vst[:, hh * 128:(hh + 1) * 128], [rvst], [R_kvmine[8 + h]]))
        groups = [[2 * g, 2 * g + 1] for g in range(n_cores // 2)]
        if mode == "fused":
            for p in range(16):
                S.add("pool", lambda e, p=p: e.collective_compute("AllGather", ALU.bypass, replica_groups=groups,
                                                                ins=[kv_mine[p].ap()], outs=[kv_gath[p].ap()]),
                      [R_kvmine[p]], [R_kvgath[p]], kind="cc", tag="allgather")

    def diff_attn(i):
        b = i - NA
        tiles = [1, 2, 3, 4]
        for t in tiles:
            rmsnorm_tile(t, f"norm_mix{i}")
        QALL = at(f"QALL{i}", [128, KC, OWN], BF16, off=PLP + 10240)
        R_qall = [[Res(f"qall{h}_{to}") for to in range(4)] for h in range(KC)]
        ET2 = [[at(f"ET2{i}_{j}_{c}", [128, 512], BF16, off=PLP + (j * 2 + c) * 1024) for c in range(2)] for j in range(2)]
        R_et2 = [[Res(f"et2{j}_{c}") for c in range(2)] for j in range(2)]
        assert PLP + 10240 + 32768 <= CS_OFF, (PLP, CS_OFF)
        for s in range(2):
            wt, rw = load_w(W[f"dq{b}"], "k", s * 512, s * 512 + 512)
            for to in range(4):
                t = to + 1
                c0, w = TILES[t]
                for mm_ in range(4):
                    h = s * 4 + mm_
                    ps, rps = bank()
                    for k in range(KC):
                        mm(ps[:, :], wt[:, k, mm_ * 128:(mm_ + 1) * 128], H[:, k, c0:c0 + w], k == 0, k == KC - 1,
                           [rw, RH[k][t]], [rps])
                    rope_from_psum(ps, rps, to, QALL[:, h, to * 512:(to + 1) * 512], [R_qall[h][to]])
        QZ = [[at(f"QZ{i}_{j}_{c}", [128, OWN], BF16, off=RING_OFF + 8192 + (j * 2 + c) * 4096) for c in range(2)] for j in range(2)]
        KB = [at(f"KB{i}_{hf}", [128, OWN], BF16, off=RING_OFF + 3 * 8192 + hf * 4096) for hf in range(2)]
        VB = [at(f"VB{i}_{hf}", [128, 16, 128], BF16, off=RING_OFF + 4 * 8192 + hf * 4096) for hf in range(2)]
        R_qz = [[[Res(f"qz{j}_{c}_{to}") for to in range(4)] for c in range(2)] for j in range(2)]
        R_kb = [Res(f"kb{hf}") for hf in range(2)]
        R_vb = [Res(f"vb{hf}") for hf in range(2)]
        for j in range(2):
            for c in range(2):
                z0 = 64 if c == 0 else 0
                S.add("dve", lambda e, ap=QZ[j][c][z0:z0 + 64, :]: e.memset(ap, 0.0), (),
                      [R_slot[1 + j]] + R_qz[j][c], tag="qzzero")
        ei = 0
        for h in range(KC):
            j = h % 2
            tk3 = [R_slot[3]] if h == 0 else []
            tk4 = [R_slot[4]] if h == 0 else []
            dma("sp", KB[0][:, :], kv_gath[h].ap()[0:128, :], [R_kvgath[h]], [R_kb[0]] + tk3)
            dma("sp", VB[0][:, :, :], kv_gath[8 + h].ap()[0:128, :].rearrange("p (b e) -> p b e", e=128),
                [R_kvgath[8 + h]], [R_vb[0]] + tk4)
            dma("sp", KB[1][:, :], kv_mine[h].ap(), [R_kvmine[h]], [R_kb[1]] + tk3)
            dma("sp", VB[1][:, :, :], kv_mine[8 + h].ap().rearrange("p (b e) -> p b e", e=128),
                [R_kvmine[8 + h]], [R_vb[1]] + tk4)
            for to in range(4):
                copy(QZ[j][0][0:64, to * 512:(to + 1) * 512], QALL[0:64, h, to * 512:(to + 1) * 512],
                     [R_qall[h][to]], [R_qz[j][0][to]], eng="act")
                copy(QZ[j][1][64:128, to * 512:(to + 1) * 512], QALL[64:128, h, to * 512:(to + 1) * 512],
                     [R_qall[h][to]], [R_qz[j][1][to]], eng="act")
            for to in range(4):
                t = to + 1
                c0, w = TILES[t]
                blocks = [(0, kb, 0, False) for kb in range(16)]
                blocks += [(1, kb, 0, False) for kb in range(4 * to)]
                blocks += [(1, 4 * to + jd, jd * 128, True) for jd in range(4)]
                bO = [bank(pin=True), bank(pin=True)]
                bL = [bank(pin=True), bank(pin=True)]
                nb = len(blocks)
                for bi, (hf, kb, off, diag) in enumerate(blocks):
                    et = ET2[ei % 2]
                    ret = R_et2[ei % 2]
                    ei += 1
                    for c in range(2):
                        ps, rps = bank()
                        mm(ps[:, off:512], KB[hf][:, kb * 128:(kb + 1) * 128], QZ[j][c][:, to * 512 + off:(to + 1) * 512],
                           True, True, [R_kb[hf], R_qz[j][c][to], R_slot[3], R_slot[1 + j]], [rps])
                        if hf == 0:
                            act(et[c][:, off:512], ps[:, off:512], AF.Exp, [rps, R_vec], [ret[c]], bias=vcol("rectb"), scale=0.125)
                        else:
                            act(et[c][:, off:512], ps[:, off:512], AF.Exp, [rps, R_vec], [ret[c]], bias=vcol("zero"), scale=0.125)
                        if diag:
                            tt(et[c][:, off:off + 128], et[c][:, off:off + 128], maskd, ALU.mult, [ret[c], R_const], [ret[c]])
                    for c in range(2):
                        mm(bO[c][0][:, off:512], VB[hf][:, kb, :], et[c][:, off:512], bi == 0, bi == nb - 1,
                           [R_vb[hf], ret[c], R_slot[4]], [bO[c][1]])
                    for c in range(2):
                        mm(bL[c][0][:, off:512], ones, et[c][:, off:512], bi == 0, bi == nb - 1,
                           [R_const, ret[c]], [bL[c][1]])
                unpin_all()
                r0, rr0 = st_next()
                recip(r0[:, :], bL[0][0][:, :], [bL[0][1]], [rr0])
                o0, ro0 = st_next()
                tt(o0[:, :], bO[0][0][:, :], r0[:, :], ALU.mult, [bO[0][1], rr0], [ro0])
                r1, rr1 = st_next()
                recip(r1[:, :], bL[1][0][:, :], [bL[1][1]], [rr1])
                o1, ro1 = st_next()
                tt(o1[:, :], bO[1][0][:, :], r1[:, :], ALU.mult, [bO[1][1], rr1], [ro1])
                stt(o0[:, :], o1[:, :], LAMV[:, 2 * b + 1:2 * b + 2], o0[:, :], ALU.mult, ALU.add, [ro1, ro0, R_lamv], [ro0])
                sq, rsq = sq_next()
                act(sq[:, :], o0[:, :], AF.Square, [ro0], [rsq])
                pss, rpss = bank()
                mm(pss[:, :], ones, sq[:, :], True, True, [R_const, rsq], [rpss])
                act(r0[:, :], pss[:, :], AF.Sqrt, [rpss, R_vec], [rr0], bias=vcol("eps_sub"), scale=1.0 / 128.0)
                recip(r0[:, :], r0[:, :], [rr0], [rr0])
                stt(H[:, h, c0:c0 + w], o0[:, :], LAMV[:, 4 + b:5 + b], r0[:, :], ALU.mult, ALU.mult,
                    [ro0, rr0, R_lamv], [RH[h][t]])
        linear_to_x(f"do{b}", tiles, H, RH)

    done = False
    layers = range(DEPTH) if mode == "fused" else (range(NA) if mode == "A" else range(NA, DEPTH))
    for i in layers:
        if i < NA:
            conv_mixer(i)
            tiles = list(range(5)) if i == 0 else [1, 2, 3, 4]
        else:
            if i == NA and mode == "fused":
                shared_kv()
                if stop_after == "kv":
                    done = True
                    break
            diff_attn(i)
            tiles = [1, 2, 3, 4]
        if stop_after == f"mix{i}":
            done = True
            break
        mem_attn(i, tiles)
        if stop_after == f"mem{i}":
            done = True
            break
        ffn(i, tiles)
        if stop_after == f"x{i}":
            done = True
            break
    if mode == "A":
        shared_kv()
        done = True
    if not done:
        for t in range(1, 5):
            c0, w = TILES[t]
            ps, rps = bank()
            for k in range(KC):
                sq, rsq = sq_next()
                act(sq[:, 0:w], X[:, k, c0:c0 + w], AF.Square, [RX[k][t]], [rsq])
                mm(ps[:, 0:w], ones, sq[:, 0:w], k == 0, k == KC - 1, [rsq, R_const], [rps])
            st, rst = rstd_from_sumsq(ps, rps, w, D, "eps_rms")
            for k in range(KC):
                stt(X[:, k, c0:c0 + w], X[:, k, c0:c0 + w], vcol("norm_final", k), st[:, 0:w], ALU.mult, ALU.mult,
                    [RX[k][t], rst, R_vec], [RX[k][t]])
    dump_and_finish()
    stats = S.emit(final_ops)
    return nc, stats


def prepare_inputs(inputs, n_cores=8):
    f = lambda k: np.asarray(inputs[k])
    x = f("x").astype(np.float32, copy=False)
    mem = f("mem").astype(np.float32, copy=False)
    pos = f("positions").astype(np.int32, copy=False)
    shared = {}
    for a in range(NA):
        shared[f"w_pw1{a}"] = _wl(f("conv_w_pw1")[a])
        shared[f"w_pw2{a}"] = _wl(f("conv_w_pw2")[a])
    shared["w_ksh"] = _wl(f("w_k_shared"))
    shared["w_vsh"] = _wl(f("w_v_shared"))
    for b in range(2):
        shared[f"w_dq{b}"] = _wl(f("diff_w_q")[b])
        shared[f"w_do{b}"] = _wl(f("diff_w_o")[b])
    for i in range(DEPTH):
        shared[f"w_mq{i}"] = _wl(f("mem_w_q")[i])
        shared[f"w_mk{i}"] = _wl(f("mem_w_k")[i])
        shared[f"w_mv{i}"] = _wl(f("mem_w_v")[i])
        shared[f"w_mo{i}"] = _wl(f("mem_w_o")[i])
        shared[f"w_fg{i}"] = _wl(f("ffn_w_gate")[i])
        shared[f"w_fu{i}"] = _wl(f("ffn_w_up")[i])
        shared[f"w_fd{i}"] = _wl(f("ffn_w_down")[i])
    vec = np.zeros((128, NV), np.float32)

    def put(name, v):
        c = _col(np.asarray(v, np.float32))
        vec[:, VCOL[name]:VCOL[name] + c.shape[1]] = c

    for i in range(DEPTH):
        put(f"norm_mix{i}", f("norm_mix")[i])
        put(f"norm_mem{i}", f("norm_mem")[i])
        put(f"norm_ffn{i}", f("norm_ffn")[i])
    put("norm_final", f("norm_final"))
    put("kv_norm", f("kv_norm"))
    for a in range(NA):
        put(f"b_pw1{a}", f("conv_b_pw1")[a])
        put(f"b_dw{a}", f("conv_b_dw")[a])
        put(f"ln_g{a}", f("conv_ln_g")[a])
        put(f"ln_b{a}", f("conv_ln_b")[a])
        put(f"b_pw2{a}", f("conv_b_pw2")[a])
        wd = f("conv_w_dw")[a]
        vec[:, VCOL[f"wdw{a}"]:VCOL[f"wdw{a}"] + CONVW * 8] = wd.reshape(CONVW, 8, 128).transpose(2, 0, 1).reshape(128, CONVW * 8)
    for b in range(2):
        vec[:, VCOL[f"subg{b}"]] = f("diff_subln_g")[b]
    inv_freq = (np.float32(500000.0) ** (-np.arange(0, 16, 2, dtype=np.float32) / np.float32(16))).astype(np.float32)
    fr = np.zeros(128, np.float32)
    for p in range(128):
        if p % 64 < 16:
            fr[p] = inv_freq[(p % 64) % 8]
    vec[:, VCOL["freq"]] = fr
    vec[:, VCOL["eps_rms"]] = RMS_EPS
    vec[:, VCOL["eps_ln"]] = LN_EPS
    vec[:, VCOL["eps_sub"]] = SUB_EPS
    vec[:, VCOL["zero"]] = 0.0
    cst = np.zeros((128, 512), np.float32)
    cst[:, 0:128] = np.eye(128, dtype=np.float32)
    cst[:, 128:256] = 1.0
    P = np.zeros((128, 128), np.float32)
    for c in range(2):
        for ii in range(8):
            P[64 * c + ii, 64 * c + ii + 8] = -1.0
            P[64 * c + ii + 8, 64 * c + ii] = 1.0
    cst[:, 256:384] = P.T
    kk = np.arange(128)[:, None]
    qq = np.arange(128)[None, :]
    cst[:, 384:512] = (qq >= kk).astype(np.float32)
    lamin = np.concatenate([np.concatenate([f("diff_lambda_q1")[b], f("diff_lambda_k1")[b],
                                            f("diff_lambda_q2")[b], f("diff_lambda_k2")[b]]) for b in range(2)])
    lamin = np.ascontiguousarray(lamin.astype(np.float32)[None, :])
    in_maps = []
    for c in range(n_cores):
        b, half = c // 2, c % 2
        t0 = half * OWN
        xs = np.zeros((TE, D), np.float32)
        if half == 0:
            xs[HALO:] = x[b, 0:OWN]
        else:
            xs[:] = x[b, t0 - HALO:t0 + OWN]
        xT = np.ascontiguousarray(xs.reshape(TE, KC, 128).transpose(2, 1, 0))
        memT = np.ascontiguousarray(mem[b].reshape(MEMT_N, KC, 128).transpose(2, 1, 0))
        v = vec.copy()
        v[:, VCOL["hv"]] = 0.0 if half == 0 else 1.0
        v[:, VCOL["rectb"]] = NEG_BIG if half == 0 else 0.0
        m = dict(shared)
        m.update(xT=xT, memT=memT, pos=np.ascontiguousarray(pos[b, t0:t0 + OWN][None, :]), vecs=v, lamin=lamin, cst=cst)
        in_maps.append(m)
    return in_maps


def assemble_output(results, n_cores=8):
    out = np.zeros((n_cores // 2, SEQ, D), np.float32)
    for c in range(n_cores):
        b, half = c // 2, c % 2
        y = np.asarray(results[c]["y"])
        out[b, half * OWN:(half + 1) * OWN, :] = y.transpose(2, 1, 0).reshape(OWN, D)
    return out


_CACHE = {}


def kernel(**inputs):
    n = 8
    if "nc" not in _CACHE:
        _CACHE["nc"] = build_program(n)[0]
    in_maps = prepare_inputs(inputs, n)
    res = run_bass_kernel_spmd(_CACHE["nc"], in_maps, core_ids=list(range(n)))
    return assemble_output(res.results, n)
```

```python
import math
import numpy as np
import concourse.bass as bass
import concourse.mybir as mybir
from concourse.bass_utils import run_bass_kernel_spmd

F32 = mybir.dt.float32
BF16 = mybir.dt.bfloat16
I32 = mybir.dt.int32
AF = mybir.ActivationFunctionType
ALU = mybir.AluOpType
AX = mybir.AxisListType

D = 1024
KC = 8
SEQ = 4096
OWN = 2048
HALO = 64
TE = OWN + HALO
FF = 2816
FC = 22
DEPTH = 4
NA = 2
CONVW = 31
MEMT_N = 256
TILES = [(0, HALO)] + [(HALO + 512 * i, 512) for i in range(4)]
OWN_TILES = TILES[1:]
FFN_GROUPS = [(0, 4), (4, 4), (8, 4), (12, 4), (16, 4), (20, 2)]
RMS_EPS = 1e-6
LN_EPS = 1e-5
SUB_EPS = 1e-5
NEG_BIG = -30000.0
NO_CC = False
KV_PARTS = ("tables", "k", "v", "kstore", "vstore")
STRICT_SYNC = True


class Res:
    __slots__ = ("name", "last_w", "readers")

    def __init__(self, name):
        self.name = name
        self.last_w = None
        self.readers = []


class Op:
    __slots__ = ("eng", "fn", "deps", "kind", "needs_signal", "sig", "tag")

    def __init__(self, eng, fn, kind, tag):
        self.eng = eng
        self.fn = fn
        self.deps = []
        self.kind = kind
        self.needs_signal = False
        self.sig = None
        self.tag = tag


class Sched:
    ENGS = ("pe", "act", "dve", "pool", "sp")
    EPOCH = 30000

    def __init__(self, nc, n_dma_sems=20):
        self.nc = nc
        self.ops = []
        self.e = {"pe": nc.tensor, "act": nc.scalar, "dve": nc.vector,
                  "pool": nc.gpsimd, "sp": nc.sync}
        self.n_dma_sems = n_dma_sems

    def last_op(self, eng):
        for op in reversed(self.ops):
            if op.eng == eng and op.kind == "c":
                return op
        return None

    def add(self, eng, fn, reads=(), writes=(), kind="c", tag="", after=()):
        op = Op(eng, fn, kind, tag)
        deps = {}
        for d in after:
            if d is not None:
                deps.setdefault(id(d), [d, True])
        for r in reads:
            if r.last_w is not None:
                deps.setdefault(id(r.last_w), [r.last_w, False])[1] = True
        for w in writes:
            if w.last_w is not None:
                deps.setdefault(id(w.last_w), [w.last_w, False])
            for rd in w.readers:
                deps.setdefault(id(rd), [rd, False])
        for d, raw in deps.values():
            if d is op:
                continue
            if d.eng == eng and d.kind == "c" and kind == "c":
                if eng == "pe" or (not raw and not STRICT_SYNC):
                    continue
            d.needs_signal = True
            op.deps.append(d)
        for r in reads:
            if kind == "c":
                r.readers = [x for x in r.readers if not (x.kind == "c" and x.eng == eng)]
            r.readers.append(op)
        for w in writes:
            w.last_w = op
            w.readers = []
        self.ops.append(op)
        return op

    def emit(self, final_wait_ops=()):
        nc = self.nc
        n_sig = {k: 0 for k in self.ENGS}
        for op in self.ops:
            if op.kind == "c" and op.needs_signal:
                n_sig[op.eng] += 1
        sems = {k: [nc.alloc_semaphore(f"s_{k}{j}") for j in range(n_sig[k] // self.EPOCH + 1)]
                for k in self.ENGS}
        nds = self.n_dma_sems
        dsems = {q: [nc.alloc_semaphore(f"s_dma_{q}{i}") for i in range(nds)] for q in ("sp", "pool")}
        cnt = {k: 0 for k in self.ENGS}
        dcnt = {q: [0] * nds for q in dsems}
        dlast = {q: [None] * nds for q in dsems}
        ndq = {q: 0 for q in dsems}
        known = {k: {} for k in self.ENGS}
        nd = 0
        nwaits = 0
        ncc = 0

        def wait(engname, sig):
            nonlocal nwaits
            sem, val, key = sig
            if known[engname].get(key, 0) >= val:
                return
            self.e[engname].wait_ge(sem, val)
            known[engname][key] = val
            nwaits += 1

        for op in self.ops:
            E = self.e[op.eng]
            for d in op.deps:
                assert d.sig is not None, (op.tag, d.tag)
                wait(op.eng, d.sig)
            if op.kind == "d":
                q = op.eng
                i = ndq[q] % nds
                ndq[q] += 1
                nd += 1
                if dlast[q][i] is not None:
                    wait(op.eng, dlast[q][i])
                ins = op.fn(E)
                dcnt[q][i] += 16
                ins.then_inc(dsems[q][i], 16)
                op.sig = (dsems[q][i], dcnt[q][i], ("d", q, i))
                dlast[q][i] = op.sig
            elif op.kind == "cc":
                s = nc.alloc_semaphore(f"s_cc{ncc}")
                ncc += 1
                ins = op.fn(E)
                ins.then_inc(s, 1)
                op.sig = (s, 1, ("cc", ncc))
            else:
                ins = op.fn(E)
                if op.needs_signal:
                    ep, c = divmod(cnt[op.eng], self.EPOCH)
                    cnt[op.eng] += 1
                    ins.then_inc(sems[op.eng][ep], 1)
                    op.sig = (sems[op.eng][ep], c + 1, (op.eng, ep))
        for op in final_wait_ops:
            wait("sp", op.sig)
        self.stats = dict(n_ops=len(self.ops), n_waits=nwaits, sig=dict(cnt), n_dma=nd)
        return self.stats


def _vec_layout():
    cols = {}
    cur = 0

    def addv(name, n):
        nonlocal cur
        cols[name] = cur
        cur += n

    for i in range(DEPTH):
        addv(f"norm_mix{i}", 8)
        addv(f"norm_mem{i}", 8)
        addv(f"norm_ffn{i}", 8)
    addv("norm_final", 8)
    addv("kv_norm", 8)
    for a in range(NA):
        addv(f"b_pw1{a}", 16)
        addv(f"b_dw{a}", 8)
        addv(f"ln_g{a}", 8)
        addv(f"ln_b{a}", 8)
        addv(f"b_pw2{a}", 8)
        addv(f"wdw{a}", CONVW * 8)
    for b in range(2):
        addv(f"subg{b}", 1)
    for nm in ("hv", "rectb", "freq", "eps_rms", "eps_ln", "eps_sub", "zero"):
        addv(nm, 1)
    return cols, cur


VCOL, NV = _vec_layout()


def _wl(w):
    k, n = w.shape
    return np.ascontiguousarray(w.reshape(k // 128, 128, n).transpose(1, 0, 2))


def _col(v):
    return v.reshape(-1, 128).T


def build_program(n_cores=8, stop_after=None, mode="fused"):
    nc = bass.Bass("TRN2", target_bir_lowering=False)
    S = Sched(nc)

    def dram_in(name, shape, dt=F32):
        return nc.dram_tensor(name, list(shape), dt, kind="ExternalInput")

    xT_d = dram_in("xT", [128, KC, TE])
    memT_d = dram_in("memT", [128, KC, MEMT_N])
    pos_d = dram_in("pos", [1, OWN], I32)
    vecs_d = dram_in("vecs", [128, NV])
    lamin_d = dram_in("lamin", [1, 512])
    cst_d = dram_in("cst", [128, 512])
    WSHAPE = {}
    for a in range(NA):
        WSHAPE[f"pw1{a}"] = [128, KC, 2 * D]
        WSHAPE[f"pw2{a}"] = [128, KC, D]
    WSHAPE["ksh"] = [128, KC, D]
    WSHAPE["vsh"] = [128, KC, D]
    for b in range(2):
        WSHAPE[f"dq{b}"] = [128, KC, D]
        WSHAPE[f"do{b}"] = [128, KC, D]
    for i in range(DEPTH):
        for nm in ("mq", "mk", "mv", "mo"):
            WSHAPE[f"{nm}{i}"] = [128, KC, D]
        WSHAPE[f"fg{i}"] = [128, KC, FF]
        WSHAPE[f"fu{i}"] = [128, KC, FF]
        WSHAPE[f"fd{i}"] = [128, FC, D]

    class _LazyW(dict):
        def __missing__(self, key):
            self[key] = dram_in("w_" + key, WSHAPE[key])
            return self[key]

    W = _LazyW()
    if stop_after is None and mode == "fused":
        for key in WSHAPE:
            W[key]
    y_d = nc.dram_tensor("y", [128, KC, OWN], F32, kind="ExternalOutput")
    if mode == "B":
        kv_mine = [nc.dram_tensor(f"kvm{p}", [128, OWN], BF16, kind="ExternalInput") for p in range(16)]
        kv_gath = [nc.dram_tensor(f"kvg{p}", [256, OWN], BF16, kind="ExternalInput") for p in range(16)]
    else:
        kv_mine = [nc.dram_tensor(f"kvm{p}", [128, OWN], BF16, kind=("ExternalOutput" if mode == "A" else "Internal")) for p in range(16)]
        kv_gath = [nc.dram_tensor(f"kvg{p}", [256, OWN], BF16, kind="Internal") for p in range(16)] if mode == "fused" else None
    R_kvmine = [Res(f"kvmine{p}") for p in range(16)]
    R_kvgath = [Res(f"kvgath{p}") for p in range(16)]

    arena = nc.alloc_sbuf_tensor("arena", [128, 212480], mybir.dt.uint8)
    base = nc.lookup_mloc(arena).addr
    cur = [base]

    def at(name, shape, dt, off=None, advance=True):
        nbytes = int(np.prod(shape[1:])) * (4 if dt in (F32, I32) else 2)
        o = cur[0] if off is None else off
        t = nc.alloc_sbuf_tensor_at(name, list(shape), dt, offset=o)
        if off is None and advance:
            cur[0] += (nbytes + 31) // 32 * 32
        return t

    X = at("X", [128, KC, TE], F32)
    H = at("H", [128, KC, TE], BF16)
    VEC = at("VEC", [128, NV], F32)
    CONST = at("CONST", [128, 512], BF16)
    NEGH = at("NEGH", [128, 512], BF16)
    LAMV = at("LAMV", [128, 16], F32)
    RING_OFF = cur[0]
    NSLOT = 5
    cur[0] += NSLOT * 8192
    PL = cur[0]
    PL_END = base + 212480
    CS_OFF = PL_END - 8192
    assert PL + 56576 <= CS_OFF, (PL, CS_OFF)

    RX = [[Res(f"X{k}_{t}") for t in range(5)] for k in range(KC)]
    RH = [[Res(f"H{k}_{t}") for t in range(5)] for k in range(KC)]
    R_vec = Res("vec")
    R_const = Res("const")
    R_negh = Res("negh")
    R_lamv = Res("lamv")
    R_slot = [Res(f"slot{i}") for i in range(NSLOT)]
    slot_flat = [at(f"slot{i}", [128, 4096], BF16, off=RING_OFF + i * 8192) for i in range(NSLOT)]
    slot_k = [at(f"slotk{i}", [128, 8, 512], BF16, off=RING_OFF + i * 8192) for i in range(NSLOT)]
    slot_d = [at(f"slotd{i}", [128, 4, 1024], BF16, off=RING_OFF + i * 8192) for i in range(NSLOT)]
    ring_i = [0]

    def ring_next():
        i = ring_i[0] % NSLOT
        ring_i[0] += 1
        return i

    ident = CONST[:, 0:128]
    ones = CONST[:, 128:256]
    PTm = CONST[:, 256:384]
    maskd = CONST[:, 384:512]

    def vcol(name, j=0):
        c = VCOL[name] + j
        return VEC[:, c:c + 1]

    SQ = [at(f"SQ{i}", [128, 512], BF16, off=PL + i * 1024) for i in range(2)]
    R_sq = [Res(f"sq{i}") for i in range(2)]
    ST = [at(f"ST{i}", [128, 512], F32, off=PL + 2048 + i * 2048) for i in range(4)]
    R_st = [Res(f"st{i}") for i in range(4)]
    PLP = PL + 10240
    sq_i = [0]
    st_i = [0]

    def sq_next():
        i = sq_i[0] % 2
        sq_i[0] += 1
        return SQ[i], R_sq[i]

    def st_next():
        i = st_i[0] % 4
        st_i[0] += 1
        return ST[i], R_st[i]

    PS = [nc.alloc_psum_tensor(f"ps{i}", [128, 512], F32) for i in range(8)]
    R_ps = [Res(f"ps{i}") for i in range(8)]
    ps_i = [0]

    pinned = set()

    def bank(pin=False):
        while True:
            i = ps_i[0] % 8
            ps_i[0] += 1
            if i not in pinned:
                break
        if pin:
            pinned.add(i)
        return PS[i], R_ps[i]

    def unpin_all():
        pinned.clear()

    def dma(eng, out, in_, reads=(), writes=(), tag="dma", after=()):
        return S.add(eng, lambda e: e.dma_start(out=out, in_=in_), reads, writes, kind="d", tag=tag, after=after)

    def mm(out, lhsT, rhs, start, stop, reads, writes):
        return S.add("pe", lambda e: e.matmul(out, lhsT, rhs, start=start, stop=stop), reads, writes, tag="mm")

    def act(out, in_, func, reads, writes, bias=None, scale=1.0):
        if bias is None:
            return S.add("act", lambda e: e.activation(out=out, in_=in_, func=func, scale=scale), reads, writes, tag="act")
        return S.add("act", lambda e: e.activation(out=out, in_=in_, func=func, bias=bias, scale=scale), reads, writes, tag="act")

    def tt(out, in0, in1, op, reads, writes, eng="dve"):
        return S.add(eng, lambda e: e.tensor_tensor(out=out, in0=in0, in1=in1, op=op), reads, writes, tag="tt")

    def ts(out, in0, s1, s2, op0, op1, reads, writes, eng="dve"):
        if op1 is None:
            return S.add(eng, lambda e: e.tensor_scalar(out=out, in0=in0, scalar1=s1, scalar2=None, op0=op0), reads, writes, tag="ts")
        return S.add(eng, lambda e: e.tensor_scalar(out=out, in0=in0, scalar1=s1, scalar2=s2, op0=op0, op1=op1), reads, writes, tag="ts")

    def stt(out, in0, scalar, in1, op0, op1, reads, writes, eng="dve"):
        return S.add(eng, lambda e: e.scalar_tensor_tensor(out=out, in0=in0, scalar=scalar, in1=in1, op0=op0, op1=op1), reads, writes, tag="stt")

    def copy(out, in_, reads, writes, eng="dve"):
        if eng == "act":
            return S.add("act", lambda e: e.activation(out=out, in_=in_, func=AF.Copy), reads, writes, tag="copy")
        return S.add(eng, lambda e: e.tensor_copy(out=out, in_=in_), reads, writes, tag="copy")

    def memset(ap, val, writes, eng="dve", after=()):
        return S.add(eng, lambda e: e.memset(ap, val), (), writes, tag="memset", after=after)

    def powm05(ap, w, res):
        return S.add("pool", lambda e: e.tensor_tensor(out=ap, in0=ap, in1=NEGH[:, 0:w], op=ALU.pow), [res, R_negh], [res], tag="pow")

    def recip(out, in_, reads, writes):
        return S.add("dve", lambda e: e.reciprocal(out=out, in_=in_), reads, writes, tag="recip")

    def load_w(wd, kind, a0, a1):
        i = ring_next()
        if kind == "k":
            dma("pool", slot_k[i][:, :, 0:a1 - a0], wd.ap()[:, :, a0:a1], (), [R_slot[i]], tag="wload")
            return slot_k[i], R_slot[i]
        dma("pool", slot_d[i][:, 0:a1 - a0, :], wd.ap()[:, a0:a1, :], (), [R_slot[i]], tag="wload")
        return slot_d[i], R_slot[i]

    def rstd_from_sumsq(ps, rps, w, n, eps_name, extra_reads=()):
        st, rst = st_next()
        act(st[:, 0:w], ps[:, 0:w], AF.Sqrt, [rps, R_vec], [rst], bias=vcol(eps_name), scale=1.0 / n)
        recip(st[:, 0:w], st[:, 0:w], [rst], [rst])
        return st, rst

    def rmsnorm_tile(t, gname, src_own_only=False):
        c0, w = TILES[t]
        ps, rps = bank()
        for k in range(KC):
            sq, rsq = sq_next()
            act(sq[:, 0:w], X[:, k, c0:c0 + w], AF.Square, [RX[k][t]], [rsq])
            mm(ps[:, 0:w], ones, sq[:, 0:w], k == 0, k == KC - 1, [rsq, R_const], [rps])
        st, rst = rstd_from_sumsq(ps, rps, w, D, "eps_rms")
        for k in range(KC):
            stt(H[:, k, c0:c0 + w], X[:, k, c0:c0 + w], vcol(gname, k), st[:, 0:w], ALU.mult, ALU.mult,
                [RX[k][t], rst, R_vec], [RH[k][t]])

    def linear_to_x(wname, tiles, src, rsrc, bias_name=None):
        for s in range(2):
            wt, rw = load_w(W[wname], "k", s * 512, s * 512 + 512)
            for t in tiles:
                c0, w = TILES[t]
                for mm_ in range(4):
                    m = s * 4 + mm_
                    ps, rps = bank()
                    for k in range(KC):
                        mm(ps[:, 0:w], wt[:, k, mm_ * 128:(mm_ + 1) * 128], src[:, k, c0:c0 + w],
                           k == 0, k == KC - 1, [rw, rsrc[k][t]], [rps])
                    if bias_name is None:
                        tt(X[:, m, c0:c0 + w], ps[:, 0:w], X[:, m, c0:c0 + w], ALU.add, [rps, RX[m][t]], [RX[m][t]])
                    else:
                        stt(X[:, m, c0:c0 + w], ps[:, 0:w], vcol(bias_name, m), X[:, m, c0:c0 + w], ALU.add, ALU.add,
                            [rps, RX[m][t], R_vec], [RX[m][t]])

    for k in range(KC):
        dma("sp", X[:, k, :], xT_d.ap()[:, k, :], (), [RX[k][t] for t in range(5)], tag="xload")
    dma("sp", VEC[:, :], vecs_d.ap(), (), [R_vec])
    dma("pool", CONST[:, :], cst_d.ap(), (), [R_const])

    LAMT = at("LAMT", [128, 512], F32, off=PLP)
    R_lamt = Res("lamt")
    dma("sp", LAMT[:, :], bass.AP(lamin_d, 0, [[0, 128], [1, 512]]), (), [R_lamt])
    LSC = at("LSC", [128, 64], F32, off=PLP + 2048)
    R_lsc = Res("lsc")
    LRED = at("LRED", [128, 8], F32, off=PLP + 2048 + 256)
    R_lred = Res("lred")
    for b in range(2):
        for j in range(2):
            o = b * 256 + j * 128
            tt(LSC[:, :], LAMT[:, o:o + 64], LAMT[:, o + 64:o + 128], ALU.mult, [R_lamt], [R_lsc])
            S.add("dve", lambda e, oo=LRED[:, 2 * b + j:2 * b + j + 1]: e.tensor_reduce(out=oo, in_=LSC[:, :], axis=AX.X, op=ALU.add),
                  [R_lsc], [R_lred], tag="lred")
    act(LRED[:, 4:8], LRED[:, 0:4], AF.Exp, [R_lred], [R_lred])
    for b in range(2):
        li = 0.8 - 0.6 * math.exp(-0.3 * (NA + b))
        tt(LAMV[:, 2 * b:2 * b + 1], LRED[:, 4 + 2 * b:5 + 2 * b], LRED[:, 5 + 2 * b:6 + 2 * b], ALU.subtract, [R_lred], [R_lamv])
        ts(LAMV[:, 2 * b:2 * b + 1], LAMV[:, 2 * b:2 * b + 1], li, None, ALU.add, None, [R_lamv], [R_lamv])
        ts(LAMV[:, 2 * b + 1:2 * b + 2], LAMV[:, 2 * b:2 * b + 1], -1.0, None, ALU.mult, None, [R_lamv], [R_lamv])
        ts(LAMV[:, 4 + b:5 + b], vcol(f"subg{b}"), 1.0 - li, None, ALU.mult, None, [R_vec, R_lamv], [R_lamv])

    final_ops = []
    kvstores = []

    def dump_and_finish():
        if mode == "A":
            final_ops.extend(kvstores)
        for k in range(KC):
            final_ops.append(dma("sp", y_d.ap()[:, k, :], X[:, k, HALO:TE], [RX[k][t] for t in range(1, 5)], ()))

    def conv_mixer(i):
        a = i
        tiles = list(range(5))
        fence = S.last_op("pe")
        for t in tiles:
            rmsnorm_tile(t, f"norm_mix{i}")
        U = at(f"U{i}", [128, KC, 32 + TE], BF16, off=PLP)
        RU = [[Res(f"U{k}_{t}") for t in range(5)] for k in range(KC)]
        RUpad = [Res(f"Upad{k}") for k in range(KC)]
        DIAG = at(f"DIAG{i}", [128, CONVW, 128], BF16, off=PLP + 34304)
        R_diag = [Res(f"diag{j}") for j in range(CONVW)]
        GATE = [at(f"GATE{i}_{j}", [128, 512], F32, off=PLP + 34304 + 7936) for j in range(2)]
        R_gate = [Res("gate0")] * 2
        T1 = [at(f"T1{i}_{j}", [128, 512], F32, off=PLP + 34304 + 7936 + 2048) for j in range(2)]
        R_t1 = [Res("t10")] * 2
        for k in range(KC):
            S.add("dve", lambda e, ap=U[:, k, 0:32]: e.memset(ap, 0.0), [RH[0][t] for t in tiles], [RUpad[k]],
                  tag="padzero", after=[fence])
        gi = 0
        for g in range(2):
            wa, rwa = load_w(W[f"pw1{a}"], "k", g * 512, g * 512 + 512)
            wg, rwg = load_w(W[f"pw1{a}"], "k", D + g * 512, D + g * 512 + 512)
            for t in tiles:
                c0, w = TILES[t]
                for mm_ in range(4):
                    m = g * 4 + mm_
                    pa, rpa = bank()
                    for k in range(KC):
                        mm(pa[:, 0:w], wa[:, k, mm_ * 128:(mm_ + 1) * 128], H[:, k, c0:c0 + w], k == 0, k == KC - 1,
                           [rwa, RH[k][t]], [rpa])
                    pg, rpg = bank()
                    for k in range(KC):
                        mm(pg[:, 0:w], wg[:, k, mm_ * 128:(mm_ + 1) * 128], H[:, k, c0:c0 + w], k == 0, k == KC - 1,
                           [rwg, RH[k][t]], [rpg])
                    gt, rgt = GATE[gi % 2], R_gate[gi % 2]
                    gi += 1
                    act(gt[:, 0:w], pg[:, 0:w], AF.Sigmoid, [rpg, R_vec], [rgt], bias=vcol(f"b_pw1{a}", 8 + m))
                    stt(U[:, m, 32 + c0:32 + c0 + w], pa[:, 0:w], vcol(f"b_pw1{a}", m), gt[:, 0:w], ALU.add, ALU.mult,
                        [rpa, rgt, R_vec], [RU[m][t]])
                    if t == 0:
                        ts(U[:, m, 32:32 + HALO], U[:, m, 32:32 + HALO], vcol("hv"), None, ALU.mult, None,
                           [RU[m][0], R_vec], [RU[m][0]])
        for k in range(KC):
            for j in range(CONVW):
                ts(DIAG[:, j, :], ident, vcol(f"wdw{a}", j * 8 + k), None, ALU.mult, None, [R_const, R_vec], [R_diag[j]])
            for t in tiles:
                c0, w = TILES[t]
                ps, rps = bank()
                rds = [RU[k][t], RUpad[k]] + ([RU[k][t - 1]] if t > 0 else [])
                for j in range(CONVW):
                    o = c0 + 2 + j
                    mm(ps[:, 0:w], DIAG[:, j, :], U[:, k, o:o + w], j == 0, j == CONVW - 1, [R_diag[j]] + rds, [rps])
                act(H[:, k, c0:c0 + w], ps[:, 0:w], AF.Identity, [rps, R_vec], [RH[k][t]], bias=vcol(f"b_dw{a}", k))
        ti = 0
        for t in tiles:
            c0, w = TILES[t]
            p1, rp1 = bank()
            p2, rp2 = bank()
            for k in range(KC):
                sq, rsq = sq_next()
                act(sq[:, 0:w], H[:, k, c0:c0 + w], AF.Square, [RH[k][t]], [rsq])
                mm(p1[:, 0:w], ones, H[:, k, c0:c0 + w], k == 0, k == KC - 1, [RH[k][t], R_const], [rp1])
                mm(p2[:, 0:w], ones, sq[:, 0:w], k == 0, k == KC - 1, [rsq, R_const], [rp2])
            mean, rmean = st_next()
            ts(mean[:, 0:w], p1[:, 0:w], 1.0 / D, None, ALU.mult, None, [rp1], [rmean])
            msq, rmsq = st_next()
            tt(msq[:, 0:w], mean[:, 0:w], mean[:, 0:w], ALU.mult, [rmean], [rmsq])
            var, rvar = st_next()
            stt(var[:, 0:w], p2[:, 0:w], 1.0 / D, msq[:, 0:w], ALU.mult, ALU.subtract, [rp2, rmsq], [rvar])
            act(var[:, 0:w], var[:, 0:w], AF.Sqrt, [rvar, R_vec], [rvar], bias=vcol("eps_ln"))
            recip(var[:, 0:w], var[:, 0:w], [rvar], [rvar])
            for k in range(KC):
                t1, rt1 = T1[ti % 2], R_t1[ti % 2]
                ti += 1
                tt(t1[:, 0:w], H[:, k, c0:c0 + w], mean[:, 0:w], ALU.subtract, [RH[k][t], rmean], [rt1])
                tt(t1[:, 0:w], t1[:, 0:w], var[:, 0:w], ALU.mult, [rt1, rvar], [rt1])
                act(H[:, k, c0:c0 + w], t1[:, 0:w], AF.Silu, [rt1, R_vec], [RH[k][t]],
                    bias=vcol(f"ln_b{a}", k), scale=vcol(f"ln_g{a}", k))
        linear_to_x(f"pw2{a}", tiles, H, RH, bias_name=f"b_pw2{a}")

    def mem_attn(i, tiles):
        QT = [at(f"QT{i}_{j}", [128, KC, 512], BF16, off=PLP + j * 8192) for j in range(2)]
        R_qt = [[Res(f"qt{j}_{k}") for k in range(KC)] for j in range(2)]
        KMT = at(f"KMT{i}", [128, KC, MEMT_N], BF16, off=PLP + 16384)
        R_kmt = [Res(f"kmt{k}") for k in range(KC)]
        VM = at(f"VM{i}", [128, 2, D], BF16, off=PLP + 20480)
        R_vm = [[Res(f"vm{mt}_{s}") for s in range(2)] for mt in range(2)]
        ET = [at(f"ET{i}_{j}", [128, 2, 512], BF16, off=PLP + 24576 + j * 2048) for j in range(2)]
        R_et = [[Res(f"et{j}_{mh}") for mh in range(2)] for j in range(2)]
        MEMT = at(f"MEMT{i}", [128, KC, MEMT_N], BF16, off=PLP + 28672)
        R_memt = Res("memt")
        dma("pool", MEMT[:, :, :], memT_d.ap(), (), [R_memt], after=[S.last_op("pe")])
        for s in range(2):
            wt, rw = load_w(W[f"mk{i}"], "k", s * 512, s * 512 + 512)
            for mm_ in range(4):
                m = s * 4 + mm_
                ps, rps = bank()
                for k in range(KC):
                    mm(ps[:, 0:MEMT_N], wt[:, k, mm_ * 128:(mm_ + 1) * 128], MEMT[:, k, :], k == 0, k == KC - 1,
                       [rw, R_memt], [rps])
                copy(KMT[:, m, :], ps[:, 0:MEMT_N], [rps], [R_kmt[m]])
        for s in range(2):
            wt, rw = load_w(W[f"mv{i}"], "k", s * 512, s * 512 + 512)
            for mt in range(2):
                ps, rps = bank()
                for k in range(KC):
                    mm(ps[:, :], MEMT[:, k, mt * 128:(mt + 1) * 128], wt[:, k, :], k == 0, k == KC - 1,
                       [rw, R_memt], [rps])
                copy(VM[:, mt, s * 512:(s + 1) * 512], ps[:, :], [rps], [R_vm[mt][s]])
        wq = [load_w(W[f"mq{i}"], "k", s * 512, s * 512 + 512) for s in range(2)]
        ei = 0
        for ti, t in enumerate(tiles):
            c0, w = TILES[t]
            rmsnorm_tile(t, f"norm_mem{i}")
            qt, rqt = QT[ti % 2], R_qt[ti % 2]
            for m in range(KC):
                wt, rw = wq[m // 4]
                ps, rps = bank()
                for k in range(KC):
                    mm(ps[:, 0:w], wt[:, k, (m % 4) * 128:(m % 4 + 1) * 128], H[:, k, c0:c0 + w], k == 0, k == KC - 1,
                       [rw, RH[k][t]], [rps])
                act(qt[:, m, 0:w], ps[:, 0:w], AF.Copy, [rps], [rqt[m]])
            for hh in range(4):
                et, ret = ET[ei % 2], R_et[ei % 2]
                ei += 1
                for mh in range(2):
                    ps, rps = bank()
                    for dd in range(2):
                        c = 2 * hh + dd
                        mm(ps[:, 0:w], KMT[:, c, mh * 128:(mh + 1) * 128], qt[:, c, 0:w], dd == 0, dd == 1,
                           [R_kmt[c], rqt[c]], [rps])
                    act(et[:, mh, 0:w], ps[:, 0:w], AF.Exp, [rps], [ret[mh]], scale=1.0 / 16.0)
                pd, rpd = bank()
                for mh in range(2):
                    mm(pd[:, 0:w], ones, et[:, mh, 0:w], mh == 0, mh == 1, [R_const, ret[mh]], [rpd])
                rd, rrd = st_next()
                recip(rd[:, 0:w], pd[:, 0:w], [rpd], [rrd])
                for dd in range(2):
                    c = 2 * hh + dd
                    po, rpo = bank()
                    for mh in range(2):
                        mm(po[:, 0:w], VM[:, mh, c * 128:(c + 1) * 128], et[:, mh, 0:w], mh == 0, mh == 1,
                           [R_vm[mh][c // 4], ret[mh]], [rpo])
                    tt(H[:, c, c0:c0 + w], po[:, 0:w], rd[:, 0:w], ALU.mult, [rpo, rrd], [RH[c][t]])
        linear_to_x(f"mo{i}", tiles, H, RH)

    def ffn(i, tiles):
        HID = [at(f"HID{i}_{j}", [128, 4, 512], BF16, off=PLP + j * 4096) for j in range(2)]
        R_hid = [[Res(f"hid{j}_{q}") for q in range(4)] for j in range(2)]
        SG = [at(f"SG{i}_{j}", [128, 512], F32, off=PLP + 8192 + j * 2048) for j in range(2)]
        R_sg = [Res(f"sg{j}") for j in range(2)]
        for t in tiles:
            rmsnorm_tile(t, f"norm_ffn{i}")
        hi = 0
        si = 0
        for (j0, gs) in FFN_GROUPS:
            wg, rwg = load_w(W[f"fg{i}"], "k", j0 * 128, (j0 + gs) * 128)
            wu, rwu = load_w(W[f"fu{i}"], "k", j0 * 128, (j0 + gs) * 128)
            wd, rwd = load_w(W[f"fd{i}"], "d", j0, j0 + gs)
            for t in tiles:
                c0, w = TILES[t]
                hid, rhid = HID[hi % 2], R_hid[hi % 2]
                hi += 1
                for jj in range(gs):
                    pg, rpg = bank()
                    for k in range(KC):
                        mm(pg[:, 0:w], wg[:, k, jj * 128:(jj + 1) * 128], H[:, k, c0:c0 + w], k == 0, k == KC - 1,
                           [rwg, RH[k][t]], [rpg])
                    pu, rpu = bank()
                    for k in range(KC):
                        mm(pu[:, 0:w], wu[:, k, jj * 128:(jj + 1) * 128], H[:, k, c0:c0 + w], k == 0, k == KC - 1,
                           [rwu, RH[k][t]], [rpu])
                    sg, rsg = SG[si % 2], R_sg[si % 2]
                    si += 1
                    act(sg[:, 0:w], pg[:, 0:w], AF.Silu, [rpg], [rsg])
                    tt(hid[:, jj, 0:w], sg[:, 0:w], pu[:, 0:w], ALU.mult, [rsg, rpu], [rhid[jj]])
                for m in range(KC):
                    ps, rps = bank()
                    for jj in range(gs):
                        mm(ps[:, 0:w], wd[:, jj, m * 128:(m + 1) * 128], hid[:, jj, 0:w], jj == 0, jj == gs - 1,
                           [rwd, rhid[jj]], [rps])
                    tt(X[:, m, c0:c0 + w], ps[:, 0:w], X[:, m, c0:c0 + w], ALU.add, [rps, RX[m][t]], [RX[m][t]])

    CS = at("CS", [128, 2, OWN], BF16, off=CS_OFF)
    R_cs = [Res(f"cs{t}") for t in range(4)]

    def rope_tables():
        PI = at("PI", [128, 512], I32, off=PLP + 16384)
        PF = at("PF", [128, 512], F32, off=PLP + 16384 + 2048)
        TF = at("TF", [128, 512], F32, off=PLP + 16384 + 4096)
        FR = at("FR", [128, 512], F32, off=PLP + 16384 + 6144)
        R_pi, R_pf, R_tf, R_fr = Res("pi"), Res("pf"), Res("tf"), Res("fr")
        fence = S.last_op("pe")
        for t in range(4):
            dma("sp", PI[:, :], bass.AP(pos_d, t * 512, [[0, 128], [1, 512]]), (), [R_pi], after=[fence])
            copy(PF[:, :], PI[:, :], [R_pi], [R_pf])
            ts(PF[:, :], PF[:, :], vcol("freq"), 1.0 / (2.0 * math.pi), ALU.mult, ALU.mult, [R_pf, R_vec], [R_pf])
            for which in (1, 0):
                if which == 0:
                    ts(PF[:, :], PF[:, :], 0.25, None, ALU.add, None, [R_pf], [R_pf])
                copy(PI[:, :], PF[:, :], [R_pf], [R_pi])
                copy(TF[:, :], PI[:, :], [R_pi], [R_tf])
                tt(FR[:, :], PF[:, :], TF[:, :], ALU.subtract, [R_pf, R_tf], [R_fr])
                act(CS[:, which, t * 512:(t + 1) * 512], FR[:, :], AF.Sin, [R_fr], [R_cs[t]], scale=2.0 * math.pi)

    rope_tables()

    KBQ = [at(f"KBQ{j}", [128, 512], BF16, off=PLP + 8192 + j * 1024) for j in range(2)]
    R_kbq = [Res(f"kbq{j}") for j in range(2)]
    rope_i = [0]

    def rope_from_psum(ps, rps, to, out_ap, out_res):
        j = rope_i[0] % 2
        rope_i[0] += 1
        kb, rkb = KBQ[j], R_kbq[j]
        t1, rt1 = st_next()
        act(kb[:, :], ps[:, :], AF.Copy, [rps], [rkb])
        p2, rp2 = bank()
        mm(p2[:, :], PTm, kb[:, :], True, True, [R_const, rkb], [rp2])
        tt(t1[:, :], ps[:, :], CS[:, 0, to * 512:(to + 1) * 512], ALU.mult, [rps, R_cs[to], rkb], [rt1])
        st, rst = st_next()
        tt(st[:, :], p2[:, :], CS[:, 1, to * 512:(to + 1) * 512], ALU.mult, [rp2, R_cs[to]], [rst])
        tt(out_ap, t1[:, :], st[:, :], ALU.add, [rt1, rst] + R_cs, out_res)

    def shared_kv():
        for t in range(1, 5):
            rmsnorm_tile(t, "kv_norm")
        KST = [at(f"KST{j}", [128, 512], BF16, off=PLP + j * 1024) for j in range(2)]
        R_kst = [Res(f"kst{j}") for j in range(2)]
        VST = [at(f"VST{j}", [128, 512], BF16, off=PLP + 2048 + j * 1024) for j in range(2)]
        R_vst = [Res(f"vst{j}") for j in range(2)]
        ki = 0
        for s in range(2 if "k" in KV_PARTS else 0):
            wt, rw = load_w(W["ksh"], "k", s * 512, s * 512 + 512)
            for to in range(4):
                t = to + 1
                c0, w = TILES[t]
                for mm_ in range(4):
                    h = s * 4 + mm_
                    ps, rps = bank()
                    for k in range(KC):
                        mm(ps[:, :], wt[:, k, mm_ * 128:(mm_ + 1) * 128], H[:, k, c0:c0 + w], k == 0, k == KC - 1,
                           [rw, RH[k][t]], [rps])
                    kst, rkst = KST[ki % 2], R_kst[ki % 2]
                    ki += 1
                    rope_from_psum(ps, rps, to, kst[:, :], [rkst])
                    if "kstore" in KV_PARTS:
                        kvstores.append(dma("sp", kv_mine[h].ap()[:, to * 512:(to + 1) * 512], kst[:, :], [rkst], [R_kvmine[h]]))
        vi = 0
        for s in range(2 if "v" in KV_PARTS else 0):
            wt, rw = load_w(W["vsh"], "k", s * 512, s * 512 + 512)
            for to in range(4):
                t = to + 1
                c0, w = TILES[t]
                for sub in range(4):
                    blk = to * 4 + sub
                    ps, rps = bank()
                    for k in range(KC):
                        mm(ps[:, :], H[:, k, c0 + sub * 128:c0 + (sub + 1) * 128], wt[:, k, :], k == 0, k == KC - 1,
                           [rw, RH[k][t]], [rps])
                    vst, rvst = VST[vi % 2], R_vst[vi % 2]
                    vi += 1
                    copy(vst[:, :], ps[:, :], [rps], [rvst])
                    for hh in range(4 if "vstore" in KV_PARTS else 0):
                        h = s * 4 + hh
                        kvstores.append(dma("sp", kv_mine[8 + h].ap()[:, blk * 128:(blk + 1) * 128],
                                            vst[:, hh * 128:(hh + 1) * 128], [rvst], [R_kvmine[8 + h]]))
        groups = [[2 * g, 2 * g + 1] for g in range(n_cores // 2)]
        if mode == "fused":
            for p in range(16):
                S.add("pool", lambda e, p=p: e.collective_compute("AllGather", ALU.bypass, replica_groups=groups,
                                                                ins=[kv_mine[p].ap()], outs=[kv_gath[p].ap()]),
                      [R_kvmine[p]], [R_kvgath[p]], kind="cc", tag="allgather")

    def diff_attn(i):
        b = i - NA
        tiles = [1, 2, 3, 4]
        for t in tiles:
            rmsnorm_tile(t, f"norm_mix{i}")
        QALL = at(f"QALL{i}", [128, KC, OWN], BF16, off=PLP + 10240)
        R_qall = [[Res(f"qall{h}_{to}") for to in range(4)] for h in range(KC)]
        ET2 = [[at(f"ET2{i}_{j}_{c}", [128, 512], BF16, off=PLP + (j * 2 + c) * 1024) for c in range(2)] for j in range(2)]
        R_et2 = [[Res(f"et2{j}_{c}") for c in range(2)] for j in range(2)]
        assert PLP + 10240 + 32768 <= CS_OFF, (PLP, CS_OFF)
        for s in range(2):
            wt, rw = load_w(W[f"dq{b}"], "k", s * 512, s * 512 + 512)
            for to in range(4):
                t = to + 1
                c0, w = TILES[t]
                for mm_ in range(4):
                    h = s * 4 + mm_
                    ps, rps = bank()
                    for k in range(KC):
                        mm(ps[:, :], wt[:, k, mm_ * 128:(mm_ + 1) * 128], H[:, k, c0:c0 + w], k == 0, k == KC - 1,
                           [rw, RH[k][t]], [rps])
                    rope_from_psum(ps, rps, to, QALL[:, h, to * 512:(to + 1) * 512], [R_qall[h][to]])
        QZ = [[at(f"QZ{i}_{j}_{c}", [128, OWN], BF16, off=RING_OFF + 8192 + (j * 2 + c) * 4096) for c in range(2)] for j in range(2)]
        KB = [at(f"KB{i}_{hf}", [128, OWN], BF16, off=RING_OFF + 3 * 8192 + hf * 4096) for hf in range(2)]
        VB = [at(f"VB{i}_{hf}", [128, 16, 128], BF16, off=RING_OFF + 4 * 8192 + hf * 4096) for hf in range(2)]
        R_qz = [[[Res(f"qz{j}_{c}_{to}") for to in range(4)] for c in range(2)] for j in range(2)]
        R_kb = [Res(f"kb{hf}") for hf in range(2)]
        R_vb = [Res(f"vb{hf}") for hf in range(2)]
        for j in range(2):
            for c in range(2):
                z0 = 64 if c == 0 else 0
                S.add("dve", lambda e, ap=QZ[j][c][z0:z0 + 64, :]: e.memset(ap, 0.0), (),
                      [R_slot[1 + j]] + R_qz[j][c], tag="qzzero")
        ei = 0
        for h in range(KC):
            j = h % 2
            tk3 = [R_slot[3]] if h == 0 else []
            tk4 = [R_slot[4]] if h == 0 else []
            dma("sp", KB[0][:, :], kv_gath[h].ap()[0:128, :], [R_kvgath[h]], [R_kb[0]] + tk3)
            dma("sp", VB[0][:, :, :], kv_gath[8 + h].ap()[0:128, :].rearrange("p (b e) -> p b e", e=128),
                [R_kvgath[8 + h]], [R_vb[0]] + tk4)
            dma("sp", KB[1][:, :], kv_mine[h].ap(), [R_kvmine[h]], [R_kb[1]] + tk3)
            dma("sp", VB[1][:, :, :], kv_mine[8 + h].ap().rearrange("p (b e) -> p b e", e=128),
                [R_kvmine[8 + h]], [R_vb[1]] + tk4)
            for to in range(4):
                copy(QZ[j][0][0:64, to * 512:(to + 1) * 512], QALL[0:64, h, to * 512:(to + 1) * 512],
                     [R_qall[h][to]], [R_qz[j][0][to]], eng="dve")
                copy(QZ[j][1][64:128, to * 512:(to + 1) * 512], QALL[64:128, h, to * 512:(to + 1) * 512],
                     [R_qall[h][to]], [R_qz[j][1][to]], eng="dve")
            for to in range(4):
                t = to + 1
                c0, w = TILES[t]
                blocks = [(0, kb, 0, False) for kb in range(16)]
                blocks += [(1, kb, 0, False) for kb in range(4 * to)]
                blocks += [(1, 4 * to + jd, jd * 128, True) for jd in range(4)]
                bO = [bank(pin=True), bank(pin=True)]
                bL = [bank(pin=True), bank(pin=True)]
                nb = len(blocks)
                for bi, (hf, kb, off, diag) in enumerate(blocks):
                    et = ET2[ei % 2]
                    ret = R_et2[ei % 2]
                    ei += 1
                    for c in range(2):
                        ps, rps = bank()
                        mm(ps[:, off:512], KB[hf][:, kb * 128:(kb + 1) * 128], QZ[j][c][:, to * 512 + off:(to + 1) * 512],
                           True, True, [R_kb[hf], R_qz[j][c][to], R_slot[3], R_slot[1 + j]], [rps])
                        if hf == 0:
                            act(et[c][:, off:512], ps[:, off:512], AF.Exp, [rps, R_vec], [ret[c]], bias=vcol("rectb"), scale=0.125)
                        else:
                            act(et[c][:, off:512], ps[:, off:512], AF.Exp, [rps, R_vec], [ret[c]], bias=vcol("zero"), scale=0.125)
                        if diag:
                            tt(et[c][:, off:off + 128], et[c][:, off:off + 128], maskd, ALU.mult, [ret[c], R_const], [ret[c]])
                    for c in range(2):
                        mm(bO[c][0][:, off:512], VB[hf][:, kb, :], et[c][:, off:512], bi == 0, bi == nb - 1,
                           [R_vb[hf], ret[c], R_slot[4]], [bO[c][1]])
                        mm(bL[c][0][:, off:512], ones, et[c][:, off:512], bi == 0, bi == nb - 1,
                           [R_const, ret[c]], [bL[c][1]])
                unpin_all()
                r0, rr0 = st_next()
                recip(r0[:, :], bL[0][0][:, :], [bL[0][1]], [rr0])
                o0, ro0 = st_next()
                tt(o0[:, :], bO[0][0][:, :], r0[:, :], ALU.mult, [bO[0][1], rr0], [ro0])
                r1, rr1 = st_next()
                recip(r1[:, :], bL[1][0][:, :], [bL[1][1]], [rr1])
                o1, ro1 = st_next()
                tt(o1[:, :], bO[1][0][:, :], r1[:, :], ALU.mult, [bO[1][1], rr1], [ro1])
                stt(o0[:, :], o1[:, :], LAMV[:, 2 * b + 1:2 * b + 2], o0[:, :], ALU.mult, ALU.add, [ro1, ro0, R_lamv], [ro0])
                sq, rsq = sq_next()
                tt(sq[:, :], o0[:, :], o0[:, :], ALU.mult, [ro0], [rsq])
                pss, rpss = bank()
                mm(pss[:, :], ones, sq[:, :], True, True, [R_const, rsq], [rpss])
                act(r0[:, :], pss[:, :], AF.Sqrt, [rpss, R_vec], [rr0], bias=vcol("eps_sub"), scale=1.0 / 128.0)
                recip(r0[:, :], r0[:, :], [rr0], [rr0])
                stt(H[:, h, c0:c0 + w], o0[:, :], LAMV[:, 4 + b:5 + b], r0[:, :], ALU.mult, ALU.mult,
                    [ro0, rr0, R_lamv], [RH[h][t]])
        linear_to_x(f"do{b}", tiles, H, RH)

    done = False
    layers = range(DEPTH) if mode == "fused" else (range(NA) if mode == "A" else range(NA, DEPTH))
    for i in layers:
        if i < NA:
            conv_mixer(i)
            tiles = list(range(5)) if i == 0 else [1, 2, 3, 4]
        else:
            if i == NA and mode == "fused":
                shared_kv()
                if stop_after == "kv":
                    done = True
                    break
            diff_attn(i)
            tiles = [1, 2, 3, 4]
        if stop_after == f"mix{i}":
            done = True
            break
        mem_attn(i, tiles)
        if stop_after == f"mem{i}":
            done = True
            break
        ffn(i, tiles)
        if stop_after == f"x{i}":
            done = True
            break
    if mode == "A":
        shared_kv()
        done = True
    if not done:
        for t in range(1, 5):
            c0, w = TILES[t]
            ps, rps = bank()
            for k in range(KC):
                sq, rsq = sq_next()
                act(sq[:, 0:w], X[:, k, c0:c0 + w], AF.Square, [RX[k][t]], [rsq])
                mm(ps[:, 0:w], ones, sq[:, 0:w], k == 0, k == KC - 1, [rsq, R_const], [rps])
            st, rst = rstd_from_sumsq(ps, rps, w, D, "eps_rms")
            for k in range(KC):
                stt(X[:, k, c0:c0 + w], X[:, k, c0:c0 + w], vcol("norm_final", k), st[:, 0:w], ALU.mult, ALU.mult,
                    [RX[k][t], rst, R_vec], [RX[k][t]])
    dump_and_finish()
    stats = S.emit(final_ops)
    return nc, stats


def prepare_inputs(inputs, n_cores=8):
    f = lambda k: np.asarray(inputs[k])
    x = f("x").astype(np.float32, copy=False)
    mem = f("mem").astype(np.float32, copy=False)
    pos = f("positions").astype(np.int32, copy=False)
    shared = {}
    for a in range(NA):
        shared[f"w_pw1{a}"] = _wl(f("conv_w_pw1")[a])
        shared[f"w_pw2{a}"] = _wl(f("conv_w_pw2")[a])
    shared["w_ksh"] = _wl(f("w_k_shared"))
    shared["w_vsh"] = _wl(f("w_v_shared"))
    for b in range(2):
        shared[f"w_dq{b}"] = _wl(f("diff_w_q")[b])
        shared[f"w_do{b}"] = _wl(f("diff_w_o")[b])
    for i in range(DEPTH):
        shared[f"w_mq{i}"] = _wl(f("mem_w_q")[i])
        shared[f"w_mk{i}"] = _wl(f("mem_w_k")[i])
        shared[f"w_mv{i}"] = _wl(f("mem_w_v")[i])
        shared[f"w_mo{i}"] = _wl(f("mem_w_o")[i])
        shared[f"w_fg{i}"] = _wl(f("ffn_w_gate")[i])
        shared[f"w_fu{i}"] = _wl(f("ffn_w_up")[i])
        shared[f"w_fd{i}"] = _wl(f("ffn_w_down")[i])
    vec = np.zeros((128, NV), np.float32)

    def put(name, v):
        c = _col(np.asarray(v, np.float32))
        vec[:, VCOL[name]:VCOL[name] + c.shape[1]] = c

    for i in range(DEPTH):
        put(f"norm_mix{i}", f("norm_mix")[i])
        put(f"norm_mem{i}", f("norm_mem")[i])
        put(f"norm_ffn{i}", f("norm_ffn")[i])
    put("norm_final", f("norm_final"))
    put("kv_norm", f("kv_norm"))
    for a in range(NA):
        put(f"b_pw1{a}", f("conv_b_pw1")[a])
        put(f"b_dw{a}", f("conv_b_dw")[a])
        put(f"ln_g{a}", f("conv_ln_g")[a])
        put(f"ln_b{a}", f("conv_ln_b")[a])
        put(f"b_pw2{a}", f("conv_b_pw2")[a])
        wd = f("conv_w_dw")[a]
        vec[:, VCOL[f"wdw{a}"]:VCOL[f"wdw{a}"] + CONVW * 8] = wd.reshape(CONVW, 8, 128).transpose(2, 0, 1).reshape(128, CONVW * 8)
    for b in range(2):
        vec[:, VCOL[f"subg{b}"]] = f("diff_subln_g")[b]
    inv_freq = (np.float32(500000.0) ** (-np.arange(0, 16, 2, dtype=np.float32) / np.float32(16))).astype(np.float32)
    fr = np.zeros(128, np.float32)
    for p in range(128):
        if p % 64 < 16:
            fr[p] = inv_freq[(p % 64) % 8]
    vec[:, VCOL["freq"]] = fr
    vec[:, VCOL["eps_rms"]] = RMS_EPS
    vec[:, VCOL["eps_ln"]] = LN_EPS
    vec[:, VCOL["eps_sub"]] = SUB_EPS
    vec[:, VCOL["zero"]] = 0.0
    cst = np.zeros((128, 512), np.float32)
    cst[:, 0:128] = np.eye(128, dtype=np.float32)
    cst[:, 128:256] = 1.0
    P = np.zeros((128, 128), np.float32)
    for c in range(2):
        for ii in range(8):
            P[64 * c + ii, 64 * c + ii + 8] = -1.0
            P[64 * c + ii + 8, 64 * c + ii] = 1.0
    cst[:, 256:384] = P.T
    kk = np.arange(128)[:, None]
    qq = np.arange(128)[None, :]
    cst[:, 384:512] = (qq >= kk).astype(np.float32)
    lamin = np.concatenate([np.concatenate([f("diff_lambda_q1")[b], f("diff_lambda_k1")[b],
                                            f("diff_lambda_q2")[b], f("diff_lambda_k2")[b]]) for b in range(2)])
    lamin = np.ascontiguousarray(lamin.astype(np.float32)[None, :])
    in_maps = []
    for c in range(n_cores):
        b, half = c // 2, c % 2
        t0 = half * OWN
        xs = np.zeros((TE, D), np.float32)
        if half == 0:
            xs[HALO:] = x[b, 0:OWN]
        else:
            xs[:] = x[b, t0 - HALO:t0 + OWN]
        xT = np.ascontiguousarray(xs.reshape(TE, KC, 128).transpose(2, 1, 0))
        memT = np.ascontiguousarray(mem[b].reshape(MEMT_N, KC, 128).transpose(2, 1, 0))
        v = vec.copy()
        v[:, VCOL["hv"]] = 0.0 if half == 0 else 1.0
        v[:, VCOL["rectb"]] = NEG_BIG if half == 0 else 0.0
        m = dict(shared)
        m.update(xT=xT, memT=memT, pos=np.ascontiguousarray(pos[b, t0:t0 + OWN][None, :]), vecs=v, lamin=lamin, cst=cst)
        in_maps.append(m)
    return in_maps


def assemble_output(results, n_cores=8):
    out = np.zeros((n_cores // 2, SEQ, D), np.float32)
    for c in range(n_cores):
        b, half = c // 2, c % 2
        y = np.asarray(results[c]["y"])
        out[b, half * OWN:(half + 1) * OWN, :] = y.transpose(2, 1, 0).reshape(OWN, D)
    return out


_CACHE = {}


def kernel(**inputs):
    n = 8
    if "nc" not in _CACHE:
        _CACHE["nc"] = build_program(n)[0]
    in_maps = prepare_inputs(inputs, n)
    res = run_bass_kernel_spmd(_CACHE["nc"], in_maps, core_ids=list(range(n)))
    return assemble_output(res.results, n)
```
